# Optimizing a Trainium2 kernel written in Bass

```python
import jax, jax.numpy as jnp
from jax import lax
import numpy as np

D_MODEL = 2048
BATCH = 4
SEQ = 4096
DEPTH = 2

EPS = 1e-6
SSD_WIDTH = D_MODEL // 2
SSD_HEAD_DIM = 64
SSD_HEADS = SSD_WIDTH // SSD_HEAD_DIM
SSD_GROUPS = 2
SSD_STATE = 128
SSD_CONV = 4
SSD_CHUNK = 128
ATT_WIDTH = D_MODEL - SSD_WIDTH
ATT_HEAD_DIM = 128
ATT_HEADS = ATT_WIDTH // ATT_HEAD_DIM
ATT_KV_HEADS = 4
IDX_HEADS = 16
IDX_HEAD_DIM = 64
IDX_TOPK_MAX = 256
Q_BLOCK = 128
D_FF = 4 * D_MODEL
MIX_WIDTH = SSD_WIDTH + ATT_WIDTH
DT_COL_SCALE = 0.1

N_Z = SSD_WIDTH
N_BC = SSD_GROUPS * SSD_STATE
CONV_CH = SSD_WIDTH + 2 * N_BC
N_DT = SSD_HEADS
N_Q = ATT_HEADS * ATT_HEAD_DIM
N_KV = ATT_KV_HEADS * ATT_HEAD_DIM
N_QI = IDX_HEADS * IDX_HEAD_DIM
N_KI = IDX_HEAD_DIM
N_WI = IDX_HEADS
IN_SIZES = (N_Z, CONV_CH, N_DT, N_Q, N_KV, N_KV, N_QI, N_KI, N_WI)
IN_WIDTH = N_Z + CONV_CH + N_DT + N_Q + 2 * N_KV + N_QI + N_KI + N_WI

kernel_name = "hymba_ssd_dsa_hybrid"


def _split(t, sizes):
    pts = [int(v) for v in np.cumsum(sizes)[:-1]]
    return jnp.split(t, pts, axis=-1)


def rmsnorm(x, g):
    xf = x.astype(jnp.float32)
    y = xf * lax.rsqrt(jnp.mean(xf * xf, axis=-1, keepdims=True) + EPS)
    return (y * g.astype(jnp.float32)).astype(x.dtype)


def causal_dwconv(u, w, bias):
    ch = u.shape[-1]
    out = lax.conv_general_dilated(
        u, w[:, None, :], window_strides=(1,), padding=((SSD_CONV - 1, 0),),
        dimension_numbers=("NWC", "WIO", "NWC"), feature_group_count=ch)
    return out + bias


def ssd_chunked(xs, dt, a, bm, cm):
    b, l, h, p = xs.shape
    g, n = bm.shape[-2:]
    r = h // g
    q = SSD_CHUNK
    c = l // q
    xdt = (xs * dt[..., None]).reshape(b, c, q, g, r, p)
    adt = (dt * a).reshape(b, c, q, g, r)
    bm = bm.reshape(b, c, q, g, n)
    cm = cm.reshape(b, c, q, g, n)
    a_cum = jnp.cumsum(adt, axis=2)
    seg = a_cum[:, :, :, None] - a_cum[:, :, None, :]
    causal = jnp.tril(jnp.ones((q, q), dtype=bool))[None, None, :, :, None, None]
    decay = jnp.exp(jnp.where(causal, seg, -jnp.inf))
    cb = jnp.einsum("bclgn,bcsgn->bclsg", cm, bm)
    y_diag = jnp.einsum("bclsg,bclsgr,bcsgrp->bclgrp", cb, decay, xdt)
    decay_state = jnp.exp(a_cum[:, :, -1:] - a_cum)
    states = jnp.einsum("bcsgn,bcsgr,bcsgrp->bcgrpn", bm, decay_state, xdt)
    chunk_decay = jnp.exp(a_cum[:, :, -1])

    def step(carry, inp):
        st, dec = inp
        return carry * dec[..., None, None] + st, carry

    init = jnp.zeros((b, g, r, p, n), dtype=xs.dtype)
    _, prev = lax.scan(step, init, (jnp.swapaxes(states, 0, 1), jnp.swapaxes(chunk_decay, 0, 1)))
    prev = jnp.swapaxes(prev, 0, 1)
    y_off = jnp.einsum("bclgn,bcgrpn,bclgr->bclgrp", cm, prev, jnp.exp(a_cum))
    return (y_diag + y_off).reshape(b, l, h, p)


def dsa_attention(q, k, v, q_idx, k_idx, w_idx):
    b, l = q.shape[:2]
    topk = min(IDX_TOPK_MAX, l // 4)
    qb = min(Q_BLOCK, l)
    nb = l // qb
    rep = ATT_HEADS // ATT_KV_HEADS
    key_pos = jnp.arange(l)

    def to_blocks(t):
        return jnp.swapaxes(t.reshape(b, nb, qb, *t.shape[2:]), 0, 1)

    def one_block(args):
        q_b, qi_b, wi_b, start = args
        t_pos = start + jnp.arange(qb)
        s = jnp.einsum("bqhd,bsd->bqhs", qi_b, k_idx) * (IDX_HEAD_DIM ** -0.5)
        score = jnp.einsum("bqhs,bqh->bqs", jax.nn.relu(s), wi_b).astype(jnp.float32)
        admissible = key_pos[None, :] <= t_pos[:, None]
        score = jnp.where(admissible[None], score, -jnp.inf)
        _, sel = lax.top_k(score, topk)
        sel_ok = sel <= t_pos[None, :, None]
        k_sel = jax.vmap(lambda kk, ii: kk[ii])(k, sel)
        v_sel = jax.vmap(lambda vv, ii: vv[ii])(v, sel)
        qg = q_b.reshape(b, qb, ATT_KV_HEADS, rep, ATT_HEAD_DIM)
        logits = jnp.einsum("bqhgd,bqkhd->bqhgk", qg, k_sel).astype(jnp.float32) * (ATT_HEAD_DIM ** -0.5)
        logits = jnp.where(sel_ok[:, :, None, None, :], logits, -jnp.inf)
        prob = jax.nn.softmax(logits, axis=-1).astype(v.dtype)
        o = jnp.einsum("bqhgk,bqkhd->bqhgd", prob, v_sel)
        return o.reshape(b, qb, ATT_HEADS * ATT_HEAD_DIM)

    starts = jnp.arange(nb, dtype=jnp.int32) * qb
    out = lax.map(one_block, (to_blocks(q), to_blocks(q_idx), to_blocks(w_idx), starts))
    return jnp.swapaxes(out, 0, 1).reshape(b, l, ATT_HEADS * ATT_HEAD_DIM)


def hybrid_layer(x, g_mix, w_in, conv_w, conv_b, dt_bias, a_log, d_skip, g_ssd,
                 g_q, g_k, g_kidx, w_out, g_mlp, w_up, w_down):
    b, l, _ = x.shape
    h = rmsnorm(x, g_mix) @ w_in
    z, xbc, dt, q, k, v, qi, ki, wi = _split(h, IN_SIZES)

    xbc = jax.nn.silu(causal_dwconv(xbc, conv_w, conv_b))
    xs, bm, cm = _split(xbc, (SSD_WIDTH, N_BC, N_BC))
    xs_f = xs.astype(jnp.float32).reshape(b, l, SSD_HEADS, SSD_HEAD_DIM)
    dt_f = jax.nn.softplus(dt.astype(jnp.float32) + dt_bias.astype(jnp.float32))
    a = -jnp.exp(a_log.astype(jnp.float32))
    y = ssd_chunked(xs_f, dt_f, a,
                    bm.astype(jnp.float32).reshape(b, l, SSD_GROUPS, SSD_STATE),
                    cm.astype(jnp.float32).reshape(b, l, SSD_GROUPS, SSD_STATE))
    y = y + d_skip.astype(jnp.float32)[:, None] * xs_f
    y = y.reshape(b, l, SSD_WIDTH) * jax.nn.silu(z.astype(jnp.float32))
    yg = y.reshape(b, l, SSD_GROUPS, SSD_WIDTH // SSD_GROUPS)
    yg = yg * lax.rsqrt(jnp.mean(yg * yg, axis=-1, keepdims=True) + EPS)
    y_ssd = (yg.reshape(b, l, SSD_WIDTH) * g_ssd.astype(jnp.float32)).astype(x.dtype)

    q = rmsnorm(q.reshape(b, l, ATT_HEADS, ATT_HEAD_DIM), g_q)
    k = rmsnorm(k.reshape(b, l, ATT_KV_HEADS, ATT_HEAD_DIM), g_k)
    v = v.reshape(b, l, ATT_KV_HEADS, ATT_HEAD_DIM)
    qi = qi.reshape(b, l, IDX_HEADS, IDX_HEAD_DIM)
    ki = rmsnorm(ki, g_kidx)
    wi = wi * (IDX_HEADS ** -0.5)
    y_att = dsa_attention(q, k, v, qi, ki, wi)

    x = x + jnp.concatenate([y_ssd, y_att], axis=-1) @ w_out

    u = rmsnorm(x, g_mlp) @ w_up
    return x + jnp.square(jax.nn.relu(u)) @ w_down


def setup_inputs(seed: int = 0) -> dict:
    key = jax.random.key(seed)
    ks = jax.random.split(key, 20)
    f32 = jnp.float32
    nrm = lambda k, shape, s: jax.random.normal(k, shape, f32) * s
    gain = lambda k, shape: 1.0 + 0.02 * jax.random.normal(k, shape, f32)

    x = jax.random.normal(ks[0], (BATCH, SEQ, D_MODEL), f32)
    col_scale = jnp.ones((IN_WIDTH,), f32).at[N_Z + CONV_CH:N_Z + CONV_CH + N_DT].set(DT_COL_SCALE)
    w_in = nrm(ks[1], (DEPTH, D_MODEL, IN_WIDTH), D_MODEL ** -0.5) * col_scale
    conv_w = nrm(ks[2], (DEPTH, SSD_CONV, CONV_CH), SSD_CONV ** -0.5)
    conv_b = nrm(ks[3], (DEPTH, CONV_CH), 0.01)
    u = jax.random.uniform(ks[4], (DEPTH, SSD_HEADS), f32)
    dt0 = jnp.exp(u * (jnp.log(0.1) - jnp.log(0.001)) + jnp.log(0.001))
    dt_bias = dt0 + jnp.log(-jnp.expm1(-dt0))
    a_log = jnp.log(jax.random.uniform(ks[5], (DEPTH, SSD_HEADS), f32, 1.0, 16.0))
    d_skip = 1.0 + 0.1 * jax.random.normal(ks[6], (DEPTH, SSD_HEADS), f32)
    return {
        "x": x,
        "g_mix": gain(ks[7], (DEPTH, D_MODEL)),
        "w_in": w_in,
        "conv_w": conv_w,
        "conv_b": conv_b,
        "dt_bias": dt_bias,
        "a_log": a_log,
        "d_skip": d_skip,
        "g_ssd": gain(ks[8], (DEPTH, SSD_WIDTH)),
        "g_q": gain(ks[9], (DEPTH, ATT_HEAD_DIM)),
        "g_k": gain(ks[10], (DEPTH, ATT_HEAD_DIM)),
        "g_kidx": gain(ks[11], (DEPTH, IDX_HEAD_DIM)),
        "w_out": nrm(ks[12], (DEPTH, MIX_WIDTH, D_MODEL), MIX_WIDTH ** -0.5),
        "g_mlp": gain(ks[13], (DEPTH, D_MODEL)),
        "w_up": nrm(ks[14], (DEPTH, D_MODEL, D_FF), D_MODEL ** -0.5),
        "w_down": nrm(ks[15], (DEPTH, D_FF, D_MODEL), D_FF ** -0.5),
    }


def reference(x, g_mix, w_in, conv_w, conv_b, dt_bias, a_log, d_skip, g_ssd,
              g_q, g_k, g_kidx, w_out, g_mlp, w_up, w_down):
    for i in range(DEPTH):
        x = hybrid_layer(x, g_mix[i], w_in[i], conv_w[i], conv_b[i], dt_bias[i], a_log[i],
                         d_skip[i], g_ssd[i], g_q[i], g_k[i], g_kidx[i], w_out[i],
                         g_mlp[i], w_up[i], w_down[i])
    return x
```

```python
import os
import numpy as np
import concourse.bass as bass
import concourse.mybir as mybir
from concourse.bass_utils import run_bass_kernel_spmd

F32 = mybir.dt.float32
BF16 = mybir.dt.bfloat16
AF = mybir.ActivationFunctionType
ALU = mybir.AluOpType
AX = mybir.AxisListType

D = 2048
DFF = 8192
EPS = 1e-6
NEG = -1.0e30
COMPUTE = ("pe", "act", "dve", "pool")


class Res:
    __slots__ = ("w", "r", "pr", "name")

    def __init__(self, name=""):
        self.w = {}
        self.r = {}
        self.pr = {}
        self.name = name


def _merge(a, b):
    o = dict(a)
    for k, v in b.items():
        if o.get(k, 0) < v:
            o[k] = v
    return o


_REC = []


class _EngProxy:
    def __init__(self, eng):
        self._eng = eng

    def __getattr__(self, name):
        real = getattr(self._eng, name)
        if not callable(real):
            return real

        def f(*a, **k):
            _REC.append((real, a, k))
            return None

        return f


class _NcProxy:
    def __init__(self, nc):
        self._nc = nc
        for n in ("tensor", "vector", "scalar", "gpsimd", "sync"):
            setattr(self, n, _EngProxy(getattr(nc, n)))

    def __getattr__(self, name):
        return getattr(self._nc, name)


def _record(fn):
    del _REC[:]
    fn()
    calls = list(_REC)
    del _REC[:]
    assert calls

    def run():
        ins = None
        for real, a, k in calls:
            ins = real(*a, **k)
        return ins

    return run


class Prog:
    def __init__(self, nc, nld=8, nst=8):
        self.nc = nc
        self.streams = {h: [] for h in ("pe", "act", "dve", "pool", "sp")}
        self.nops = {e: 0 for e in COMPUTE}
        self.needed = {e: set() for e in COMPUTE}
        self.waited = {h: {} for h in self.streams}
        self.dmaq = {
            "ld": dict(host="sp", k=nld, n=0, cnt=[0] * nld),
            "st": dict(host="pool", k=nst, n=0, cnt=[0] * nst),
            "cv": dict(host="pool", k=3, n=0, cnt=[0] * 3),
        }

    def _deps(self, eng, reads, writes, awrites):
        deps = {}

        def add(k, v, kind):
            if k == eng and eng == "pe":
                return
            if deps.get(k, 0) < v:
                deps[k] = v

        for r in reads:
            for k, v in r.w.items():
                add(k, v, "raw")
        for w in writes:
            w.pr = _merge(w.w, w.r)
            w.w = {}
            w.r = {}
            for k, v in w.pr.items():
                add(k, v, "war")
        for w in awrites:
            if w.r:
                w.pr = _merge(w.w, w.r)
                w.w = {}
                w.r = {}
            for k, v in w.pr.items():
                add(k, v, "war")
        return deps

    def _emit(self, host, deps, fn, inc):
        waits = []
        wt = self.waited[host]
        for k, v in deps.items():
            if wt.get(k, 0) >= v:
                continue
            wt[k] = v
            waits.append((k, v))
            if k in COMPUTE:
                self.needed[k].add(v)
        self.streams[host].append((waits, fn, inc))

    def op(self, eng, fn, reads=(), writes=(), awrites=()):
        deps = self._deps(eng, reads, writes, awrites)
        self.nops[eng] += 1
        idx = self.nops[eng]
        self._emit(eng, deps, _record(fn), ("c", eng, idx))
        for r in reads:
            if r.r.get(eng, 0) < idx:
                r.r[eng] = idx
        for w in list(writes) + list(awrites):
            if w.w.get(eng, 0) < idx:
                w.w[eng] = idx

    def dma(self, q, fn, reads=(), writes=(), awrites=()):
        Q = self.dmaq[q]
        slot = Q["n"] % Q["k"]
        Q["n"] += 1
        deps = self._deps(q, reads, writes, awrites)
        key = (q, slot)
        if Q["cnt"][slot] > 0:
            deps[key] = max(deps.get(key, 0), Q["cnt"][slot] * 16)
        Q["cnt"][slot] += 1
        val = Q["cnt"][slot] * 16
        self._emit(Q["host"], deps, _record(fn), ("d", q, slot))
        for r in reads:
            if r.r.get(key, 0) < val:
                r.r[key] = val
        for w in list(writes) + list(awrites):
            if w.w.get(key, 0) < val:
                w.w[key] = val

    def finish(self):
        deps = {}
        for q, Q in self.dmaq.items():
            for s in range(Q["k"]):
                if Q["cnt"][s]:
                    deps[(q, s)] = Q["cnt"][s] * 16
        for e in COMPUTE:
            if self.nops[e]:
                deps[e] = self.nops[e]
        self._emit("sp", deps, None, None)

    def emit(self, sems):
        nc = self.nc
        rank = {}
        for e in COMPUTE:
            rank[e] = {v: i + 1 for i, v in enumerate(sorted(self.needed[e]))}
        engs = {"pe": "tensor", "act": "scalar", "dve": "vector", "pool": "gpsimd", "sp": "sync"}
        with nc.Block() as block:
            for host, attr in engs.items():
                stream = self.streams[host]

                def body(eng, stream=stream, host=host):
                    for waits, fn, inc in stream:
                        for k, v in waits:
                            if k in COMPUTE:
                                eng.wait_ge(sems[k], rank[k][v])
                            else:
                                eng.wait_ge(sems[k], v)
                        if fn is None:
                            continue
                        ins = fn()
                        if inc[0] == "c":
                            if inc[2] in self.needed[inc[1]]:
                                ins.then_inc(sems[inc[1]], 1)
                        else:
                            ins.then_inc(sems[(inc[1], inc[2])], 16)

                getattr(block, attr)(body)


N_Z, N_XBC, N_DT, N_Q, N_KV, N_QI, N_KI, N_WI = 1024, 1536, 16, 1024, 512, 1024, 64, 16
O_Z = 0
O_XBC = O_Z + N_Z
O_DT = O_XBC + N_XBC
O_Q = O_DT + N_DT
O_K = O_Q + N_Q
O_V = O_K + N_KV
O_QI = O_V + N_KV
O_KI = O_QI + N_QI
O_WI = O_KI + N_KI
IN_W = O_WI + N_WI

INBLK = []
for i in range(2):
    INBLK.append(("z%d" % i, [(O_Z + 512 * i, 512, 0)]))
for i in range(3):
    INBLK.append(("xbc%d" % i, [(O_XBC + 512 * i, 512, 0)]))
for i in range(2):
    INBLK.append(("q%d" % i, [(O_Q + 512 * i, 512, 0)]))
INBLK.append(("k0", [(O_K, 512, 0)]))
INBLK.append(("v0", [(O_V, 512, 0)]))
for i in range(2):
    INBLK.append(("qi%d" % i, [(O_QI + 512 * i, 512, 0)]))
INBLK.append(("misc", [(O_KI, 64, 0), (O_KI, 64, 64), (O_DT, 16, 128), (O_WI, 16, 144), (O_Z, 352, 160)]))
NB_IN = len(INBLK)
NB_OUT = 4
NB_UP = 16
NB_DN = 16
NB_LAYER = NB_IN + NB_OUT + NB_UP + NB_DN
B_IN, B_OUT, B_UP, B_DN = 0, NB_IN, NB_IN + NB_OUT, NB_IN + NB_OUT + NB_UP


def build(L=4096, NL=2, dbg=False, stop_after=None):
    nc = bass.Bass("TRN2", target_bir_lowering=False)
    TT = 512
    NT = L // TT
    NCH = L // 128
    ikind = "ExternalOutput" if dbg else "Internal"

    def din(name, shape, dt=F32):
        return nc.dram_tensor(name, list(shape), dt, kind="ExternalInput").ap()

    def dscr(name, shape, dt=BF16):
        return nc.dram_tensor(name, list(shape), dt, kind=ikind).ap()

    xT_in = din("xT", [D, L])
    w_in = din("w_in", [NL, D, IN_W])
    w_out = din("w_out", [NL, D, D])
    w_up = din("w_up", [NL, D, DFF])
    w_down = din("w_down", [NL, DFF, D])
    g_mix = din("g_mix", [NL, D])
    g_mlp = din("g_mlp", [NL, D])
    conv_w = din("conv_w", [NL, 4, 1536])
    conv_b = din("conv_b", [NL, 1536])
    dt_bias = din("dt_bias", [NL, 16])
    a_log = din("a_log", [NL, 16])
    d_skip = din("d_skip", [NL, 16])
    g_ssd = din("g_ssd", [NL, 1024])
    g_q = din("g_q", [NL, 128])
    g_k = din("g_k", [NL, 128])
    g_kidx = din("g_kidx", [NL, 64])
    outT = nc.dram_tensor("outT", [D, L], F32, kind="ExternalOutput").ap()

    WB = dscr("WB", [NL * NB_LAYER, 128, 16 * 512])
    XT = dscr("XTs", [D, L], F32)
    Zs = dscr("Zs", [L, 1024])
    Vs = dscr("Vs", [L, 512])
    DW = dscr("DWs", [L, 32], F32)
    XSs = dscr("XSs", [L, 1024])
    Bs = dscr("Bs", [L, 256])
    BCT = dscr("BCT", [512, L])
    QTs = dscr("QTs", [1024, L])
    KTs = dscr("KTs", [512, L])
    QITs = dscr("QITs", [1024, L])
    KITs = dscr("KITs", [128, L])
    YTs = dscr("YTs", [D, L])

    P = Prog(nc)
    real_nc = nc
    nc = _NcProxy(real_nc)
    sb = {}

    import contextlib

    stack = contextlib.ExitStack()

    def S(name, shape, dt=F32):
        t = stack.enter_context(nc.sbuf_tensor(name, list(shape), dt))
        return t

    def PS(name, shape, dt=F32):
        return stack.enter_context(nc.psum_tensor(name, list(shape), dt))

    with stack:
        pb = [PS("pb%d" % i, [128, 512], F32) for i in range(6)]
        pbr = [Res("pb%d" % i) for i in range(6)]
        ptrs = [PS("ptr%d" % i, [128, 1024], BF16) for i in range(2)]

        ident_f = S("ident_f", [128, 128], F32)
        ident = S("ident", [128, 128], BF16)
        ones_bf = S("ones_bf", [128, 128], BF16)
        blk_bf = S("blk_bf", [128, 128], BF16)
        ones_f = S("ones_f", [128, 128], F32)
        epsT = S("epsT", [128, 1], F32)
        cres = Res("consts")

        P.op("pool", lambda: nc.gpsimd.memset(ones_f[:], 1.0), awrites=[cres])
        P.op("pool", lambda: nc.gpsimd.memset(ones_bf[:], 1.0), awrites=[cres])
        P.op("pool", lambda: nc.gpsimd.memset(epsT[:], EPS), awrites=[cres])
        blk_r = Res("blk")
        P.op("pool", lambda: nc.gpsimd.memset(blk_bf[:], 0.0), writes=[blk_r])
        P.op("pool", lambda: nc.gpsimd.memset(blk_bf[0:64, 0:64], 1.0), writes=[blk_r])
        P.op("pool", lambda: nc.gpsimd.memset(blk_bf[64:128, 64:128], 1.0), writes=[blk_r])
        idf_r = Res("idf")
        P.op("pool", lambda: nc.gpsimd.affine_select(out=ident_f[:], in_=ones_f[:], pattern=[[-1, 128]],
                                                     compare_op=ALU.is_equal, fill=0.0, base=0, channel_multiplier=1),
             reads=[cres], writes=[idf_r])
        P.op("pool", lambda: nc.gpsimd.tensor_copy(out=ident[:], in_=ident_f[:]), reads=[idf_r, blk_r], awrites=[cres])

        gmixT = S("gmixT", [128, NL, 16], F32)
        gmlpT = S("gmlpT", [128, NL, 16], F32)
        cwT = S("cwT", [128, NL, 12, 4], F32)
        cbT = S("cbT", [128, NL, 12], F32)
        gqT = S("gqT", [128, NL], F32)
        gkT = S("gkT", [128, NL], F32)
        gkiT = S("gkiT", [128, NL], F32)
        pres = Res("params")
        with nc.allow_non_contiguous_dma(reason="tiny param loads"):
            pass

        def ldp(out, in_):
            P.dma("ld", lambda: nc.sync.dma_start(out=out, in_=in_, allow_slow_non_contiguous=True),
                  awrites=[pres])

        for l in range(NL):
            ldp(gmixT[:, l, :], g_mix[l].rearrange("(k p) -> p k", p=128))
            ldp(gmlpT[:, l, :], g_mlp[l].rearrange("(k p) -> p k", p=128))
            for j in range(4):
                ldp(cwT[:, l, :, j], conv_w[l, j].rearrange("(c p) -> p c", p=128))
            ldp(cbT[:, l, :], conv_b[l].rearrange("(c p) -> p c", p=128))
            ldp(gqT[:, l:l + 1], g_q[l].rearrange("(p o) -> p o", o=1))
            ldp(gkT[:, l:l + 1], g_k[l].rearrange("(p o) -> p o", o=1))
            ldp(gkiT[0:64, l:l + 1], g_kidx[l].rearrange("(p o) -> p o", o=1))
            ldp(gkiT[64:128, l:l + 1], g_kidx[l].rearrange("(p o) -> p o", o=1))
        P.op("dve", lambda: nc.vector.tensor_scalar(out=gqT[:], in0=gqT[:], scalar1=float(128 ** -0.5),
                                                    scalar2=None, op0=ALU.mult),
             reads=[pres], awrites=[pres])

        pst = contextlib.ExitStack()
        S0 = S
        WBr = [Res("WB%d" % i) for i in range(NL * NB_LAYER)]
        castq = []
        for l in range(NL):
            base = l * NB_LAYER
            for bi, (nm, segs) in enumerate(INBLK):
                pcs = [(w_in[l, :, so:so + n].rearrange("(k p) j -> p k j", p=128), dc, n) for so, n, dc in segs]
                castq.append((base + B_IN + bi, pcs))
            for bi in range(NB_OUT):
                castq.append((base + B_OUT + bi,
                              [(w_out[l, :, bi * 512:(bi + 1) * 512].rearrange("(k p) j -> p k j", p=128), 0, 512)]))
            for qd in range(4):
                for bi in range(4):
                    castq.append((base + B_UP + qd * 4 + bi,
                                  [(w_up[l, :, (qd * 4 + bi) * 512:(qd * 4 + bi + 1) * 512].rearrange("(k p) j -> p k j", p=128), 0, 512)]))
                for mw in range(4):
                    castq.append((base + B_DN + qd * 4 + mw,
                                  [(w_down[l, qd * 2048:(qd + 1) * 2048, mw * 512:(mw + 1) * 512]
                                    .rearrange("(k p) j -> p k j", p=128), 0, 512)]))
        castpos = [0]
        cast_done = set()

        def pump(n):
            for _ in range(n):
                if castpos[0] >= len(castq):
                    return
                bidx, pcs = castq[castpos[0]]
                castpos[0] += 1
                cast_done.add(bidx)
                for src, dc, nn in pcs:
                    P.dma("cv", lambda: nc.gpsimd.dma_start(
                        out=WB[bidx].rearrange("p (k j) -> p k j", j=512)[:, :, dc:dc + nn], in_=src,
                        allow_slow_non_contiguous=(nn < 128)), awrites=[WBr[bidx]])

        pump(3)
        pst.close()
        S = S0

        def barrier():
            deps = {}
            for e in COMPUTE:
                if P.nops[e]:
                    deps[e] = P.nops[e]
            for q, Q in P.dmaq.items():
                for s_ in range(Q["k"]):
                    if Q["cnt"][s_]:
                        deps[(q, s_)] = Q["cnt"][s_] * 16
            for host in P.streams:
                dd = {k: v for k, v in deps.items() if k != host}
                P._emit(host, dd, None, None)

        wbuf = [None] * 4
        wbuf_r = [Res("wbuf%d" % i) for i in range(4)]
        wlimit = [0]

        def walloc(stk, tag, count):
            for i in range(4):
                wbuf[i] = stk.enter_context(nc.sbuf_tensor("wbuf%d_%s" % (i, tag), [128, 16, 512], BF16))
            assert wpos[0] == wpos[1]
            wlimit[0] = wpos[0] + count
        plan = []
        for l in range(NL):
            base = l * NB_LAYER
            for ti in range(NT):
                plan += [base + B_IN + b for b in range(NB_IN)]
            for ti in range(NT):
                plan += [base + B_OUT + b for b in range(NB_OUT)]
                for qd in range(4):
                    plan += [base + B_UP + qd * 4 + b for b in range(4)]
                    plan += [base + B_DN + qd * 4 + b for b in range(4)]
        wpos = [0, 0]

        def wissue():
            n = wpos[1]
            if n >= wlimit[0]:
                return
            wpos[1] += 1
            sl = n % 4
            b = plan[n]
            while b not in cast_done:
                pump(1)
            P.dma("ld", lambda: nc.sync.dma_start(out=wbuf[sl][:], in_=WB[b].rearrange("p (k j) -> p k j", j=512)),
                  reads=[WBr[b]], writes=[wbuf_r[sl]])

        def wget(b):
            n = wpos[0]
            assert plan[n] == b, (n, plan[n], b)
            while wpos[1] < min(n + 3, wlimit[0]):
                wissue()
            wpos[0] += 1
            return wbuf[n % 4], wbuf_r[n % 4]

        NTr = lambda nm: [Res("%s%d" % (nm, i)) for i in range(NT)]
        XTr, Zr, Vr, DWr, XSr, Br, BCTr, QTr, KTr, QITr, KITr, YTr = [NTr(n) for n in
            ("XT", "Z", "V", "DW", "XS", "B", "BCT", "QT", "KT", "QIT", "KIT", "YT")]
        bankrot = [0]

        def nbank():
            i = bankrot[0] % 4
            bankrot[0] += 1
            return pb[i], pbr[i]

        nrot = [0]

        def nnbank():
            i = 4 + nrot[0] % 2
            nrot[0] += 1
            return pb[i], pbr[i]

        trot = [0]

        tb_only0 = [False]

        def tbank():
            i = 0 if tb_only0[0] else trot[0] % 2
            trot[0] += 1
            return ptrs[i][:, 0:512], ptr_rr[i]

        ptr_rr = [Res("ptr0"), Res("ptr1")]

        def mm(out, lhsT, rhs, start, stop, reads, bank_r):
            P.op("pe", lambda: nc.tensor.matmul(out, lhsT, rhs, start=start, stop=stop), reads=reads, awrites=[bank_r])

        def phase_in(l):
            base = l * NB_LAYER
            xsrc = xT_in if l == 0 else XT
            st2 = contextlib.ExitStack()

            def S2(name, shape, dt=F32):
                return st2.enter_context(nc.sbuf_tensor("%s_%d" % (name, l), list(shape), dt))

            with st2:
                walloc(st2, "in%d" % l, NT * NB_IN)
                xin = S2("xin", [128, 16, TT]); xin_r = Res()
                xn = [S2("xn%d" % i, [128, 16, TT], BF16) for i in range(2)]; xn_r = [Res(), Res()]
                sqb = [S2("sqb%d" % i, [128, TT], BF16) for i in range(2)]; sqb_r = [Res(), Res()]
                std = S2("std", [128, TT]); std_r = Res()
                rstd = S2("rstd", [128, TT]); rstd_r = Res()
                U = [S2("U%d" % i, [128, TT + 3]) for i in range(2)]; U_r = [Res(), Res()]
                acc = [S2("acc%d" % i, [128, TT]) for i in range(2)]; acc_r = [Res(), Res()]
                sbf = [S2("sbf%d" % i, [128, TT], BF16) for i in range(2)]; sbf_r = [Res(), Res()]
                qst = [S2("qst%d" % i, [128, TT], BF16) for i in range(2)]; qst_r = [Res(), Res()]
                carry = S2("carry", [128, 12, 3]); carry_r = [Res() for _ in range(12)]
                XStile = S2("XStile", [128, 4, 1024], BF16); XStile_r = Res()
                Btile = S2("Btile", [128, 4, 256], BF16); Btile_r = Res()
                Ztile = S2("Ztile", [128, 4, 1024], BF16); Ztile_r = Res()
                Vtile = S2("Vtile", [128, 4, 512], BF16); Vtile_r = Res()
                DWtile = S2("DWtile", [128, 4, 32]); DWtile_r = Res()
                cnt = {"sq": 0, "u": 0, "q": 0}
                pend = []

                def colsum_rstd(src_ap, src_r, lhs_ones, scale):
                    i = cnt["sq"] % 2; cnt["sq"] += 1
                    P.op("act", lambda: nc.scalar.activation(out=sqb[i][:], in_=src_ap, func=AF.Square),
                         reads=[src_r], writes=[sqb_r[i]])
                    nb, nb_r = nnbank()
                    mm(nb[:], lhs_ones[:], sqb[i][:], True, True, [sqb_r[i], cres], nb_r)
                    P.op("act", lambda: nc.scalar.activation(out=std[:], in_=nb[:], func=AF.Sqrt,
                                                             bias=epsT[:, 0:1], scale=scale),
                         reads=[nb_r, cres], writes=[std_r])
                    P.op("dve", lambda: nc.vector.reciprocal(out=rstd[:], in_=std[:]), reads=[std_r], writes=[rstd_r])

                def norm_tile(ti):
                    t0 = ti * TT
                    xr = [] if l == 0 else [XTr[ti]]
                    P.dma("ld", lambda: nc.sync.dma_start(
                        out=xin[:], in_=xsrc[:, t0:t0 + TT].rearrange("(k p) t -> p k t", p=128)),
                        reads=xr, writes=[xin_r])
                    nb, nb_r = nnbank()
                    for kc in range(16):
                        i = cnt["sq"] % 2; cnt["sq"] += 1
                        P.op("act", lambda kc=kc, i=i: nc.scalar.activation(out=sqb[i][:], in_=xin[:, kc, :], func=AF.Square),
                             reads=[xin_r], writes=[sqb_r[i]])
                        mm(nb[:], ones_bf[:], sqb[i][:], kc == 0, kc == 15, [sqb_r[i], cres], nb_r)
                    P.op("act", lambda: nc.scalar.activation(out=std[:], in_=nb[:], func=AF.Sqrt,
                                                             bias=epsT[:, 0:1], scale=1.0 / D),
                         reads=[nb_r, cres], writes=[std_r])
                    P.op("dve", lambda: nc.vector.reciprocal(out=rstd[:], in_=std[:]), reads=[std_r], writes=[rstd_r])
                    Xn = xn[ti % 2]; Xn_r = xn_r[ti % 2]
                    for kc in range(16):
                        P.op("dve", lambda kc=kc: nc.vector.scalar_tensor_tensor(
                            out=Xn[:, kc, :], in0=xin[:, kc, :], scalar=gmixT[:, l, kc:kc + 1], in1=rstd[:],
                            op0=ALU.mult, op1=ALU.mult),
                            reads=[xin_r, rstd_r, pres], awrites=[Xn_r])

                norm_tile(0)
                for ti in range(NT):
                    t0 = ti * TT
                    pump(1)
                    X = xn[ti % 2]; X_r = xn_r[ti % 2]

                    def flush():
                        fs = list(pend)
                        del pend[:]
                        for f in fs:
                            f()

                    def fm_chunk(wb, wb_r, sub):
                        bk, bk_r = nbank()
                        for kc in range(16):
                            mm(bk[:], wb[:, kc, sub * 128:(sub + 1) * 128], X[:, kc, :], kc == 0, kc == 15, [wb_r, X_r], bk_r)
                        flush()
                        return bk, bk_r

                    def tm_chunk(wb, wb_r, tsub, c0, n):
                        bk, bk_r = nbank()
                        for kc in range(16):
                            mm(bk[:, 0:n], X[:, kc, tsub * 128:(tsub + 1) * 128], wb[:, kc, c0:c0 + n], kc == 0, kc == 15,
                               [wb_r, X_r], bk_r)
                        flush()
                        return bk, bk_r

                    def transposes(src, src_r, dst_view, dst_r):
                        tb, tb_r = tbank()
                        for tsub in range(4):
                            P.op("pe", lambda tsub=tsub: nc.tensor.transpose(
                                out=tb[:, tsub * 128:(tsub + 1) * 128], in_=src[:, tsub * 128:(tsub + 1) * 128], identity=ident[:]),
                                reads=[src_r, cres], awrites=[tb_r])
                        P.op("act", lambda: nc.scalar.copy(out=dst_view, in_=tb.rearrange("p (s c) -> p s c", c=128)),
                             reads=[tb_r], awrites=[dst_r])

                    for zb in range(2):
                        wb, wb_r = wget(base + B_IN + zb)
                        for tsub in range(4):
                            bk, bk_r = tm_chunk(wb, wb_r, tsub, 0, 512)
                            P.op("act", lambda bk=bk, tsub=tsub, zb=zb: nc.scalar.copy(
                                out=Ztile[:, tsub, zb * 512:(zb + 1) * 512], in_=bk[:]),
                                reads=[bk_r], awrites=[Ztile_r])
                    P.dma("st", lambda: nc.gpsimd.dma_start(
                        out=Zs[t0:t0 + TT, :].rearrange("(s p) c -> p s c", p=128), in_=Ztile[:]),
                        reads=[Ztile_r], writes=[Zr[ti]])
                    if ti + 1 < NT:
                        norm_tile(ti + 1)
                    for xb in range(3):
                        wb, wb_r = wget(base + B_IN + 2 + xb)
                        for sub in range(4):
                            cc = xb * 4 + sub
                            bk, bk_r = fm_chunk(wb, wb_r, sub)
                            ui = cnt["u"] % 2; cnt["u"] += 1
                            u, u_r, ac, ac_r, sf, sf_r = U[ui], U_r[ui], acc[ui], acc_r[ui], sbf[ui], sbf_r[ui]
                            if ti == 0:
                                P.op("dve", lambda u=u: nc.vector.memset(u[:, 0:3], 0.0), writes=[u_r])
                            else:
                                P.op("dve", lambda u=u, cc=cc: nc.vector.tensor_copy(out=u[:, 0:3], in_=carry[:, cc, :]),
                                     reads=[carry_r[cc]], writes=[u_r])
                            P.op("act", lambda u=u, bk=bk: nc.scalar.copy(out=u[:, 3:3 + TT], in_=bk[:]),
                                 reads=[bk_r], awrites=[u_r])
                            P.op("dve", lambda u=u, ac=ac, cc=cc: nc.vector.tensor_scalar(
                                out=ac[:], in0=u[:, 0:TT], scalar1=cwT[:, l, cc, 0:1], scalar2=cbT[:, l, cc:cc + 1],
                                op0=ALU.mult, op1=ALU.add), reads=[u_r, pres], writes=[ac_r])
                            for j in range(1, 4):
                                P.op("dve", lambda u=u, ac=ac, cc=cc, j=j: nc.vector.scalar_tensor_tensor(
                                    out=ac[:], in0=u[:, j:j + TT], scalar=cwT[:, l, cc, j:j + 1], in1=ac[:],
                                    op0=ALU.mult, op1=ALU.add), reads=[u_r, ac_r, pres], awrites=[ac_r])
                            P.op("dve", lambda u=u, cc=cc: nc.vector.tensor_copy(out=carry[:, cc, :], in_=u[:, TT:TT + 3]),
                                 reads=[u_r], writes=[carry_r[cc]])
                            P.op("act", lambda ac=ac, sf=sf: nc.scalar.activation(out=sf[:], in_=ac[:], func=AF.Silu),
                                 reads=[ac_r], writes=[sf_r])
                            def tail(cc=cc, sf=sf, sf_r=sf_r, ti=ti, t0=t0):
                                if cc < 8:
                                    transposes(sf, sf_r, XStile[:, :, cc * 128:(cc + 1) * 128], XStile_r)
                                elif cc < 10:
                                    transposes(sf, sf_r, Btile[:, :, (cc - 8) * 128:(cc - 7) * 128], Btile_r)
                                if cc >= 8:
                                    P.dma("st", lambda: nc.gpsimd.dma_start(
                                        out=BCT[(cc - 8) * 128:(cc - 7) * 128, t0:t0 + TT], in_=sf[:]),
                                        reads=[sf_r], awrites=[BCTr[ti]])
                                if cc == 7:
                                    P.dma("st", lambda: nc.gpsimd.dma_start(
                                        out=XSs[t0:t0 + TT, :].rearrange("(s p) c -> p s c", p=128), in_=XStile[:]),
                                        reads=[XStile_r], writes=[XSr[ti]])
                                if cc == 9:
                                    P.dma("st", lambda: nc.gpsimd.dma_start(
                                        out=Bs[t0:t0 + TT, :].rearrange("(s p) c -> p s c", p=128), in_=Btile[:]),
                                        reads=[Btile_r], writes=[Br[ti]])

                            pend.append(tail)

                    def qk_chunk(wb, wb_r, sub, gT, lhs_ones, scale, dst, dst_r):
                        bk, bk_r = fm_chunk(wb, wb_r, sub)
                        i = cnt["sq"] % 2; cnt["sq"] += 1
                        P.op("act", lambda: nc.scalar.activation(out=sqb[i][:], in_=bk[:], func=AF.Square),
                             reads=[bk_r], writes=[sqb_r[i]])

                        def tail():
                            nb, nb_r = nnbank()
                            mm(nb[:], lhs_ones[:], sqb[i][:], True, True, [sqb_r[i], cres], nb_r)
                            P.op("act", lambda: nc.scalar.activation(out=std[:], in_=nb[:], func=AF.Sqrt,
                                                                     bias=epsT[:, 0:1], scale=scale),
                                 reads=[nb_r, cres], writes=[std_r])
                            P.op("dve", lambda: nc.vector.reciprocal(out=rstd[:], in_=std[:]), reads=[std_r], writes=[rstd_r])
                            qi_ = cnt["q"] % 2; cnt["q"] += 1
                            P.op("dve", lambda: nc.vector.scalar_tensor_tensor(
                                out=qst[qi_][:], in0=bk[:], scalar=gT[:, l:l + 1], in1=rstd[:], op0=ALU.mult, op1=ALU.mult),
                                reads=[bk_r, rstd_r, pres], writes=[qst_r[qi_]])
                            P.dma("st", lambda: nc.gpsimd.dma_start(out=dst, in_=qst[qi_][:]),
                                  reads=[qst_r[qi_]], awrites=[dst_r])

                        pend.append(tail)

                    for qb in range(2):
                        wb, wb_r = wget(base + B_IN + 5 + qb)
                        for sub in range(4):
                            hh = qb * 4 + sub
                            qk_chunk(wb, wb_r, sub, gqT, ones_bf, 1.0 / 128, QTs[hh * 128:(hh + 1) * 128, t0:t0 + TT], QTr[ti])
                    wb, wb_r = wget(base + B_IN + 7)
                    for sub in range(4):
                        qk_chunk(wb, wb_r, sub, gkT, ones_bf, 1.0 / 128, KTs[sub * 128:(sub + 1) * 128, t0:t0 + TT], KTr[ti])
                    wb, wb_r = wget(base + B_IN + 8)
                    for tsub in range(4):
                        bk, bk_r = tm_chunk(wb, wb_r, tsub, 0, 512)
                        P.op("act", lambda bk=bk, tsub=tsub: nc.scalar.copy(out=Vtile[:, tsub, :], in_=bk[:]),
                             reads=[bk_r], awrites=[Vtile_r])
                    P.dma("st", lambda: nc.gpsimd.dma_start(
                        out=Vs[t0:t0 + TT, :].rearrange("(s p) c -> p s c", p=128), in_=Vtile[:]),
                        reads=[Vtile_r], writes=[Vr[ti]])
                    for qb in range(2):
                        wb, wb_r = wget(base + B_IN + 9 + qb)
                        for sub in range(4):
                            hh = qb * 4 + sub
                            bk, bk_r = fm_chunk(wb, wb_r, sub)
                            qi_ = cnt["q"] % 2; cnt["q"] += 1
                            P.op("act", lambda bk=bk, qi_=qi_: nc.scalar.mul(out=qst[qi_][:], in_=bk[:], mul=1.0 / 32),
                                 reads=[bk_r], writes=[qst_r[qi_]])
                            P.dma("st", lambda qi_=qi_, hh=hh: nc.gpsimd.dma_start(
                                out=QITs[hh * 128:(hh + 1) * 128, t0:t0 + TT], in_=qst[qi_][:]),
                                reads=[qst_r[qi_]], awrites=[QITr[ti]])
                    wb, wb_r = wget(base + B_IN + 11)
                    qk_chunk(wb, wb_r, 0, gkiT, blk_bf, 1.0 / 64, KITs[:, t0:t0 + TT], KITr[ti])
                    for tsub in range(4):
                        bk, bk_r = tm_chunk(wb, wb_r, tsub, 128, 32)
                        P.op("act", lambda bk=bk, tsub=tsub: nc.scalar.copy(out=DWtile[:, tsub, :], in_=bk[:, 0:32]),
                             reads=[bk_r], awrites=[DWtile_r])
                    P.dma("st", lambda: nc.gpsimd.dma_start(
                        out=DW[t0:t0 + TT, :].rearrange("(s p) c -> p s c", p=128), in_=DWtile[:]),
                        reads=[DWtile_r], writes=[DWr[ti]])
                    flush()
                barrier()


        def phase_out(l):
            base = l * NB_LAYER
            xsrc = xT_in if l == 0 else XT
            xdst = outT if l == NL - 1 else XT
            st4 = contextlib.ExitStack()

            def S4(name, shape, dt=F32):
                return st4.enter_context(nc.sbuf_tensor("%s_o%d" % (name, l), list(shape), dt))

            with st4:
                walloc(st4, "out%d" % l, NT * (NB_OUT + NB_UP + NB_DN))
                yt = S4("yt", [128, 16, TT], BF16); yt_r = Res()
                x1 = S4("x1", [128, 16, TT]); x1_r = Res()
                xn2 = S4("xn2", [128, 16, TT], BF16); xn2_r = Res()
                actT = [S4("actT%d" % i, [128, 16, TT], BF16) for i in range(2)]; actT_r = [Res(), Res()]
                rl = [S4("rl%d" % i, [128, TT]) for i in range(2)]; rl_r = [Res(), Res()]
                sqb = [S4("sqb%d" % i, [128, TT], BF16) for i in range(2)]; sqb_r = [Res(), Res()]
                std = S4("std", [128, TT]); std_r = Res()
                rstd = S4("rstd", [128, TT]); rstd_r = Res()
                c2 = {"sq": 0, "rl": 0}
                for ti in range(NT):
                    t0 = ti * TT
                    P.dma("ld", lambda: nc.sync.dma_start(
                        out=yt[:], in_=YTs[:, t0:t0 + TT].rearrange("(k p) t -> p k t", p=128)),
                        reads=[YTr[ti]], writes=[yt_r])
                    xr = [] if l == 0 else [XTr[ti]]
                    P.dma("ld", lambda: nc.sync.dma_start(
                        out=x1[:], in_=xsrc[:, t0:t0 + TT].rearrange("(k p) t -> p k t", p=128)),
                        reads=xr, writes=[x1_r])
                    for ob in range(4):
                        wb, wb_r = wget(base + B_OUT + ob)
                        for sub in range(4):
                            m = ob * 4 + sub
                            bk, bk_r = nbank()
                            for kc in range(16):
                                mm(bk[:], wb[:, kc, sub * 128:(sub + 1) * 128], yt[:, kc, :], kc == 0, kc == 15, [wb_r, yt_r], bk_r)
                            P.op("dve", lambda: nc.vector.tensor_tensor(out=x1[:, m, :], in0=bk[:], in1=x1[:, m, :], op=ALU.add),
                                 reads=[bk_r, x1_r], awrites=[x1_r])
                    nb, nb_r = nnbank()
                    for kc in range(16):
                        i = c2["sq"] % 2; c2["sq"] += 1
                        P.op("act", lambda: nc.scalar.activation(out=sqb[i][:], in_=x1[:, kc, :], func=AF.Square),
                             reads=[x1_r], writes=[sqb_r[i]])
                        mm(nb[:], ones_bf[:], sqb[i][:], kc == 0, kc == 15, [sqb_r[i], cres], nb_r)
                    P.op("act", lambda: nc.scalar.activation(out=std[:], in_=nb[:], func=AF.Sqrt,
                                                             bias=epsT[:, 0:1], scale=1.0 / D),
                         reads=[nb_r, cres], writes=[std_r])
                    P.op("dve", lambda: nc.vector.reciprocal(out=rstd[:], in_=std[:]), reads=[std_r], writes=[rstd_r])
                    for kc in range(16):
                        P.op("dve", lambda: nc.vector.scalar_tensor_tensor(
                            out=xn2[:, kc, :], in0=x1[:, kc, :], scalar=gmlpT[:, l, kc:kc + 1], in1=rstd[:],
                            op0=ALU.mult, op1=ALU.mult),
                            reads=[x1_r, rstd_r, pres], awrites=[xn2_r])
                    for qd in range(4):
                        A = actT[qd % 2]; A_r = actT_r[qd % 2]
                        for ub in range(4):
                            wb, wb_r = wget(base + B_UP + qd * 4 + ub)
                            for sub in range(4):
                                f = ub * 4 + sub
                                bk, bk_r = nbank()
                                for kc in range(16):
                                    mm(bk[:], wb[:, kc, sub * 128:(sub + 1) * 128], xn2[:, kc, :], kc == 0, kc == 15,
                                       [wb_r, xn2_r], bk_r)
                                ri = c2["rl"] % 2; c2["rl"] += 1
                                P.op("act", lambda: nc.scalar.activation(out=rl[ri][:], in_=bk[:], func=AF.Relu),
                                     reads=[bk_r], writes=[rl_r[ri]])
                                P.op("pool", lambda: nc.gpsimd.tensor_tensor(out=A[:, f, :], in0=rl[ri][:], in1=rl[ri][:], op=ALU.mult),
                                     reads=[rl_r[ri]], awrites=[A_r])
                        for mw in range(4):
                            wb, wb_r = wget(base + B_DN + qd * 4 + mw)
                            for sub in range(4):
                                m = mw * 4 + sub
                                bk, bk_r = nbank()
                                for fc in range(16):
                                    mm(bk[:], wb[:, fc, sub * 128:(sub + 1) * 128], A[:, fc, :], fc == 0, fc == 15,
                                       [wb_r, A_r], bk_r)
                                P.op("dve", lambda: nc.vector.tensor_tensor(out=x1[:, m, :], in0=bk[:], in1=x1[:, m, :], op=ALU.add),
                                     reads=[bk_r, x1_r], awrites=[x1_r])
                    P.dma("st", lambda: nc.gpsimd.dma_start(
                        out=xdst[:, t0:t0 + TT].rearrange("(k p) t -> p k t", p=128), in_=x1[:]),
                        reads=[x1_r], writes=[XTr[ti]])
                barrier()

        def phase_att(l):
            st5 = contextlib.ExitStack()

            def S5(name, shape, dt=F32):
                return st5.enter_context(nc.sbuf_tensor("%s_a%d" % (name, l), list(shape), dt))

            with st5:
                KT = S5("KT", [128, 4, L], BF16); kt_r = Res()
                Vx = S5("Vx", [128, NCH, 4, 130], BF16); vx_r = Res()
                KI = [S5("KI%d" % k_, [128, L], BF16) for k_ in range(2)]; ki_r = Res()
                dwall = S5("dwall", [128, NCH, 32]); dw_r = Res()
                MBq = S5("MBq", [128, 128]); zf = S5("zf", [128, 128]); zbf = S5("zbf", [128, 128], BF16)
                negc = S5("negc", [128, 1]); thr_c = S5("thr_c", [128, 1])
                k_r = Res()
                for h in range(4):
                    P.dma("ld", lambda: nc.sync.dma_start(out=KT[:, h, :], in_=KTs[h * 128:(h + 1) * 128, :]),
                          reads=KTr, awrites=[kt_r])
                P.op("pool", lambda: nc.gpsimd.memset(KI[0][64:128, :], 0.0), awrites=[ki_r])
                P.op("pool", lambda: nc.gpsimd.memset(KI[1][0:64, :], 0.0), awrites=[ki_r])
                P.dma("ld", lambda: nc.sync.dma_start(out=KI[0][0:64, :], in_=KITs[0:64, :]), reads=KITr, awrites=[ki_r])
                P.dma("ld", lambda: nc.sync.dma_start(out=KI[1][64:128, :], in_=KITs[64:128, :]), reads=KITr, awrites=[ki_r])
                P.dma("ld", lambda: nc.sync.dma_start(out=dwall[:], in_=DW.rearrange("(c p) j -> p c j", p=128)),
                      reads=DWr, writes=[dw_r])
                for c0 in range(NCH):
                    P.dma("ld", lambda: nc.sync.dma_start(
                        out=Vx[:, c0, :, 0:128],
                        in_=Vs[c0 * 128:(c0 + 1) * 128, :].rearrange("p (h d) -> p h d", d=128)),
                        reads=Vr, awrites=[vx_r])

                zf_r = Res()
                P.op("pool", lambda: nc.gpsimd.memset(Vx[:, :, :, 128:130], 1.0), awrites=[vx_r])
                P.op("pool", lambda: nc.gpsimd.memset(zf[:], 0.0), writes=[zf_r])
                P.op("pool", lambda: nc.gpsimd.memset(zbf[:], 0.0), awrites=[k_r])
                P.op("pool", lambda: nc.gpsimd.memset(negc[:], -8.0), awrites=[k_r])
                P.op("pool", lambda: nc.gpsimd.memset(thr_c[:], -1.0e29), awrites=[k_r])
                P.op("pool", lambda: nc.gpsimd.affine_select(out=MBq[:], in_=zf[:], pattern=[[-1, 128]], compare_op=ALU.is_ge,
                                                             fill=NEG, base=0, channel_multiplier=1), reads=[zf_r], awrites=[k_r])
                qt = [S5("qt%d" % i, [128, 8, 128], BF16) for i in range(3)]; qt_r = [Res(), Res(), Res()]
                qit = [S5("qit%d" % i, [128, 8, 128], BF16) for i in range(2)]; qit_r = [Res(), Res()]
                scs = [S5("sc%d" % i, [128, L]) for i in range(2)]; sc_rs = [Res(), Res()]
                sel = S5("sel", [128, L], BF16); sel_r = Res()
                selT = S5("selT", [128, NCH, 128], BF16); selT_r = Res()
                m8 = S5("m8", [128, 8]); m8_r = Res()
                pT = [S5("pT%d" % i, [128, 2, 128], BF16) for i in range(3)]; pT_r = [Res(), Res(), Res()]
                nbig = S5("nbig", [128, 1])
                P.op("pool", lambda: nc.gpsimd.memset(nbig[:], -30000.0), awrites=[k_r])
                rd = S5("rd", [128, 2, 1]); rd_r = Res()
                yatt = S5("yatt", [128, 1024], BF16); yatt_r = Res()
                yat = S5("yat", [128, 8, 512], BF16); yat_r = Res()
                ca = {"rl": 0, "sb": 0, "lb": 0, "p": 0, "ob": 0}
                selTs = [selT, S5("selT1", [128, NCH, 128], BF16)]; selT_rs = [selT_r, Res()]
                KB = 20
                sbk = [pb[0], pb[1], ptrs[1][:, :].bitcast(F32)]
                sbk_r = [pbr[0], pbr[1], ptr_rr[1]]
                tb_only0[0] = True
                Dg = [S5("Dg%d" % k_, [128, 16, 128], BF16) for k_ in range(2)]; Dg_r = [Res(), Res()]
                rlb = [S5("rlb%d" % k_, [128, 512], BF16) for k_ in range(3)]; rlb_r = [Res() for _ in range(3)]
                pw = S5("pw", [128, KB + 1]); Wt = S5("Wt", [128, KB + 1])
                bhi = S5("bhi", [128, 1]); blo = S5("blo", [128, 1]); mid = S5("mid", [128, 1])
                cnt = S5("cnt", [128, 1]); dd = S5("dd", [128, 1]); bs_r = Res()

                for kk in range(KB + 1):
                    P.op("pool", lambda: nc.gpsimd.memset(pw[:, kk:kk + 1], float(2.0 ** -(kk + 1))), awrites=[k_r])
                osb = [S5("osb%d" % k_, [128, 260]) for k_ in range(4)]; osb_r = [Res() for _ in range(4)]

                def idx(i):
                    nk = (i + 1) * 128
                    ts_ = slice(i * 128, (i + 1) * 128)
                    ti = i // 4
                    sc = scs[i % 2]; sc_r = sc_rs[i % 2]
                    Q_, Q_r, QI_, QI_r = qt[i % 3], qt_r[i % 3], qit[i % 2], qit_r[i % 2]
                    P.dma("ld", lambda: nc.sync.dma_start(out=Q_[:], in_=QTs[:, ts_].rearrange("(h p) t -> p h t", p=128)),
                          reads=[QTr[ti]], writes=[Q_r])
                    P.dma("ld", lambda: nc.sync.dma_start(out=QI_[:], in_=QITs[:, ts_].rearrange("(h p) t -> p h t", p=128)),
                          reads=[QITr[ti]], writes=[QI_r])
                    Dg_ = Dg[i % 2]; Dg_r_ = Dg_r[i % 2]
                    P.op("dve", lambda: nc.vector.tensor_tensor(
                        out=Dg_[:], in0=ident[:, None, :].broadcast_to([128, 16, 128]),
                        in1=dwall[:, i, 16:32, None].broadcast_to([128, 16, 128]), op=ALU.mult),
                        reads=[cres, dw_r], writes=[Dg_r_])
                    isteps = [(k0, min(512, nk - k0), h) for k0 in range(0, nk, 512) for h in range(16)]
                    sb_of = {}

                    def emitS(s_):
                        k0, kn, h = isteps[s_]
                        sbi = ca["sb"] % 3; ca["sb"] += 1
                        sb_of[s_] = sbi
                        mm(sbk[sbi][:, 0:kn], QI_[:, h // 2, :], KI[h % 2][:, k0:k0 + kn], True, True,
                           [QI_r, ki_r], sbk_r[sbi])

                    emitS(0)
                    if len(isteps) > 1:
                        emitS(1)
                    for s_, (k0, kn, h) in enumerate(isteps):
                        if s_ + 2 < len(isteps):
                            emitS(s_ + 2)
                        sbi = sb_of.pop(s_)
                        ri = ca["rl"] % 3; ca["rl"] += 1
                        P.op("act", lambda: nc.scalar.activation(out=rlb[ri][:, 0:kn], in_=sbk[sbi][:, 0:kn], func=AF.Relu),
                             reads=[sbk_r[sbi]], writes=[rlb_r[ri]])
                        mm(pb[5][:, 0:kn], Dg_[:, h, :], rlb[ri][:, 0:kn], h == 0, h == 15, [Dg_r_, rlb_r[ri]], pbr[5])
                        if h == 15:
                            P.op("act", lambda: nc.scalar.copy(out=sc[:, k0:k0 + kn], in_=pb[5][:, 0:kn]), reads=[pbr[5]], awrites=[sc_r])

                def bisect(i):
                    nk = (i + 1) * 128
                    ts_ = slice(i * 128, (i + 1) * 128)
                    sc = scs[i % 2]; sc_r = sc_rs[i % 2]
                    work = sel; work_r = sel_r
                    P.op("dve", lambda: nc.vector.tensor_tensor(out=sc[:, ts_], in0=sc[:, ts_], in1=MBq[:], op=ALU.add),
                         reads=[sc_r, k_r], awrites=[sc_r])
                    if i <= 1:
                        thr = thr_c[:, 0:1]
                    else:
                        P.op("dve", lambda: nc.vector.tensor_reduce(out=bhi[:], in_=sc[:, 0:nk], axis=AX.X, op=ALU.max),
                             reads=[sc_r], writes=[bs_r])
                        P.op("dve", lambda: nc.vector.tensor_reduce(out=blo[:], in_=sc[:, 0:i * 128], axis=AX.X, op=ALU.min),
                             reads=[sc_r], awrites=[bs_r])
                        P.op("dve", lambda: nc.vector.tensor_tensor(out=bhi[:], in0=bhi[:], in1=blo[:], op=ALU.subtract),
                             reads=[bs_r], awrites=[bs_r])
                        P.op("dve", lambda: nc.vector.tensor_scalar(out=Wt[:], in0=pw[:], scalar1=bhi[:, 0:1], scalar2=None, op0=ALU.mult),
                             reads=[bs_r, k_r], awrites=[bs_r])
                        P.op("dve", lambda: nc.vector.tensor_tensor(out=mid[:], in0=blo[:], in1=Wt[:, 0:1], op=ALU.add),
                             reads=[bs_r], awrites=[bs_r])
                        for kk in range(KB):
                            P.op("dve", lambda: nc.vector.tensor_scalar(
                                out=work[:, 0:nk], in0=sc[:, 0:nk], scalar1=mid[:, 0:1], scalar2=None, op0=ALU.is_ge, op1=ALU.add,
                                accum_out=cnt[:, 0:1]), reads=[sc_r, bs_r], awrites=[work_r, bs_r])
                            P.op("dve", lambda: nc.vector.tensor_scalar(out=dd[:], in0=cnt[:], scalar1=255.5, scalar2=0.5,
                                                                        op0=ALU.is_gt, op1=ALU.subtract),
                                 reads=[bs_r], awrites=[bs_r])
                            P.op("dve", lambda: nc.vector.scalar_tensor_tensor(out=mid[:], in0=dd[:], scalar=Wt[:, kk:kk + 1], in1=mid[:],
                                                                               op0=ALU.mult, op1=ALU.add),
                                 reads=[bs_r], awrites=[bs_r])
                        P.op("dve", lambda: nc.vector.scalar_tensor_tensor(out=m8[:, 7:8], in0=Wt[:, KB:KB + 1], scalar=-3.0, in1=mid[:],
                                                                           op0=ALU.mult, op1=ALU.add),
                             reads=[bs_r], writes=[m8_r])
                        thr = m8[:, 7:8]
                    P.op("dve", lambda: nc.vector.tensor_scalar(out=sel[:, 0:nk], in0=sc[:, 0:nk], scalar1=thr, scalar2=None, op0=ALU.is_ge),
                         reads=[sc_r, m8_r, k_r], awrites=[sel_r])

                def selT_build(i):
                    ST, ST_r = selTs[i % 2], selT_rs[i % 2]
                    for j0 in range(0, i + 1, 4):
                        nj = min(4, i + 1 - j0)
                        tb, tb_r = tbank()
                        for jj in range(nj):
                            j = j0 + jj
                            P.op("pe", lambda: nc.tensor.transpose(out=tb[:, jj * 128:(jj + 1) * 128], in_=sel[:, j * 128:(j + 1) * 128],
                                                                   identity=ident[:]),
                                 reads=[sel_r, cres], awrites=[tb_r])
                        P.op("act", lambda: nc.scalar.activation(out=ST[:, j0:j0 + nj, :],
                                                                 in_=tb[:, 0:nj * 128].rearrange("p (k t) -> p k t", t=128),
                                                                 func=AF.Identity, bias=nbig[:, 0:1], scale=30000.0),
                             reads=[tb_r, k_r], awrites=[ST_r])

                def attn(i):
                    Q_, Q_r = qt[i % 3], qt_r[i % 3]
                    ST, ST_r = selTs[i % 2], selT_rs[i % 2]
                    steps = [(kvh, j) for kvh in range(4) for j in range(i + 1)]
                    lbs = {}

                    def emitL(s_):
                        kvh, j = steps[s_]
                        lb = 2 + ca["lb"] % 2; ca["lb"] += 1
                        lbs[s_] = lb
                        mm(pb[lb][:, 0:256], KT[:, kvh, j * 128:(j + 1) * 128],
                           Q_[:, 2 * kvh:2 * kvh + 2, :].rearrange("p h t -> p (h t)"), True, False, [kt_r, Q_r], pbr[lb])
                        mm(pb[lb][:, 0:256], ident[:], ST[:, j, None, :].broadcast_to([128, 2, 128]), False, True, [cres, ST_r], pbr[lb])

                    emitL(0)
                    O = O_r = None
                    for s_, (kvh, j) in enumerate(steps):
                        if s_ + 1 < len(steps):
                            emitL(s_ + 1)
                        lb = lbs.pop(s_)
                        if j == 0:
                            ob = 4
                            O = pb[ob]; O_r = pbr[ob]
                            mm(O[:, 0:260], zbf[:], KT[:, 0, 0:260], True, False, [k_r, kt_r], O_r)
                        pi = ca["p"] % 3; ca["p"] += 1
                        P.op("act", lambda: nc.scalar.activation(out=pT[pi][:].rearrange("p h t -> p (h t)"), in_=pb[lb][:, 0:256],
                                                                 func=AF.Exp, bias=negc[:, 0:1], scale=1.0),
                             reads=[pbr[lb], k_r], writes=[pT_r[pi]])
                        for hh in range(2):
                            mm(O[:, hh * 130:hh * 130 + 129], pT[pi][:, hh, :], Vx[:, j, kvh, 0:129], False,
                               (j == i and hh == 1), [pT_r[pi], vx_r], O_r)
                        if j == i:
                            P.op("act", lambda: nc.scalar.copy(out=osb[kvh][:], in_=O[:, 0:260]), reads=[O_r], writes=[osb_r[kvh]])

                def finalize(i):
                    ti = i // 4
                    for kvh in range(4):
                        Ov = osb[kvh][:].rearrange("p (h d) -> p h d", d=130)
                        P.op("dve", lambda: nc.vector.reciprocal(out=rd[:], in_=Ov[:, :, 128:129]), reads=[osb_r[kvh]], writes=[rd_r])
                        for hh in range(2):
                            hd = 2 * kvh + hh
                            P.op("dve", lambda: nc.vector.tensor_scalar(
                                out=yatt[:, hd * 128:(hd + 1) * 128], in0=osb[kvh][:, hh * 130:hh * 130 + 128], scalar1=rd[:, hh, :],
                                scalar2=None, op0=ALU.mult),
                                reads=[osb_r[kvh], rd_r], awrites=[yatt_r])
                    for half in range(2):
                        tb, tb_r = tbank()
                        for q4 in range(4):
                            hd = half * 4 + q4
                            P.op("pe", lambda: nc.tensor.transpose(out=tb[:, q4 * 128:(q4 + 1) * 128], in_=yatt[:, hd * 128:(hd + 1) * 128],
                                                                   identity=ident[:]),
                                 reads=[yatt_r, cres], awrites=[tb_r])
                        P.op("act", lambda: nc.scalar.copy(out=yat[:, half * 4:half * 4 + 4, (i % 4) * 128:(i % 4 + 1) * 128],
                                                           in_=tb.rearrange("p (k t) -> p k t", t=128)),
                             reads=[tb_r], awrites=[yat_r])
                    if i % 4 == 3:
                        P.dma("st", lambda: nc.gpsimd.dma_start(
                            out=YTs[1024:2048, ti * 512:(ti + 1) * 512].rearrange("(k p) t -> p k t", p=128), in_=yat[:]),
                            reads=[yat_r], awrites=[YTr[ti]])

                idx(0)
                bisect(0)
                selT_build(0)
                if NCH > 1:
                    idx(1)
                for i in range(NCH):
                    pump(3)
                    if i + 2 < NCH:
                        idx(i + 2)
                    if i + 1 < NCH:
                        bisect(i + 1)
                    attn(i)
                    finalize(i)
                    if i + 1 < NCH:
                        selT_build(i + 1)
                tb_only0[0] = False
                barrier()

        def phase_ssd(l):
            st3 = contextlib.ExitStack()

            def S3(name, shape, dt=F32):
                return st3.enter_context(nc.sbuf_tensor("%s_s%d" % (name, l), list(shape), dt))

            with st3:
                NC16 = NCH * 16
                BT = S3("BT", [128, 2, L], BF16); CT = S3("CT", [128, 2, L], BF16); bc_r = Res()
                dwall = S3("dwall", [128, NCH, 32]); dw_r = Res()
                UTf = S3("UTf", [128, 128]); MB = S3("MB", [128, 128]); zf = S3("zf", [128, 128])
                dtb = S3("dtb", [128, 16]); alb = S3("alb", [128, 16]); dsk = S3("dsk", [128, 16])
                gsb = S3("gsb", [128, 1024])
                k_r = Res()
                for g in range(2):
                    P.dma("ld", lambda: nc.sync.dma_start(out=BT[:, g, :], in_=BCT[g * 128:(g + 1) * 128, :]),
                          reads=BCTr, awrites=[bc_r])
                    P.dma("ld", lambda: nc.sync.dma_start(out=CT[:, g, :], in_=BCT[256 + g * 128:256 + (g + 1) * 128, :]),
                          reads=BCTr, awrites=[bc_r])
                P.dma("ld", lambda: nc.sync.dma_start(out=dwall[:], in_=DW.rearrange("(c p) j -> p c j", p=128)),
                      reads=DWr, writes=[dw_r])
                for dst, src in ((dtb, dt_bias), (alb, a_log), (dsk, d_skip), (gsb, g_ssd)):
                    P.dma("ld", lambda: nc.sync.dma_start(out=dst[:], in_=src[l].partition_broadcast(128)),
                          awrites=[k_r])

                zf_r = Res()
                P.op("pool", lambda: nc.gpsimd.memset(zf[:], 0.0), writes=[zf_r])
                P.op("pool", lambda: nc.gpsimd.affine_select(out=UTf[:], in_=ones_f[:], pattern=[[1, 128]], compare_op=ALU.is_ge,
                                                             fill=0.0, base=0, channel_multiplier=-1), reads=[cres], awrites=[k_r])
                P.op("pool", lambda: nc.gpsimd.affine_select(out=MB[:], in_=zf[:], pattern=[[1, 128]], compare_op=ALU.is_ge,
                                                             fill=-30000.0, base=0, channel_multiplier=-1), reads=[zf_r], awrites=[k_r])
                LIM = int(os.environ.get("SSD_LIM", "99"))

                class _Stop(Exception):
                    pass

                def ck(n):
                    if LIM <= n:
                        raise _Stop()
                ssd_body(S3, locals())
                barrier()

        def ssd_body(S3, env):
            globals_ = env
            try:
                _ssd_body(S3, env)
            except Exception as e:
                if type(e).__name__ != "_Stop":
                    raise

        def _ssd_body(S3, env):
            if True:
                (BT, CT, bc_r, dwall, dw_r, UTf, MB, dtb, alb, dsk, gsb, k_r, ck, l) = [env[n] for n in
                    ("BT", "CT", "bc_r", "dwall", "dw_r", "UTf", "MB", "dtb", "alb", "dsk", "gsb", "k_r", "ck", "l")]
                NC16 = NCH * 16
                ck(0)
                dtx = S3("dtx", [128, NCH, 16]); t_a = S3("t_a", [128, NCH, 16]); t_b = S3("t_b", [128, NCH, 16])
                dtv = S3("dtv", [128, NCH, 16]); adt = S3("adt", [128, NCH, 16]); cum = S3("cum", [128, NCH, 16])
                tot = S3("tot", [128, NCH, 16]); eal = S3("eal", [128, NCH, 16]); wse = S3("wse", [128, NCH, 16])
                cdv = S3("cdv", [128, NCH, 16]); eab = S3("eab", [128, 16])
                pr_r = Res()
                bc16 = lambda t: t[:, None, :].broadcast_to([128, NCH, 16])
                P.op("dve", lambda: nc.vector.tensor_tensor(out=dtx[:], in0=dwall[:, :, 0:16], in1=bc16(dtb), op=ALU.add),
                     reads=[dw_r, k_r], writes=[pr_r])
                P.op("act", lambda: nc.scalar.activation(out=t_a[:], in_=dtx[:], func=AF.Abs),
                     reads=[pr_r], awrites=[pr_r])
                P.op("act", lambda: nc.scalar.activation(out=t_a[:], in_=t_a[:], func=AF.Exp, scale=-1.0),
                     reads=[pr_r], awrites=[pr_r])
                P.op("act", lambda: nc.scalar.activation(out=t_a[:], in_=t_a[:], func=AF.Ln, bias=ones_f[:, 0:1], scale=1.0),
                     reads=[pr_r, cres], awrites=[pr_r])
                P.op("dve", lambda: nc.vector.tensor_scalar(out=t_b[:], in0=dtx[:], scalar1=0.0, scalar2=None, op0=ALU.max),
                     reads=[pr_r], awrites=[pr_r])
                P.op("dve", lambda: nc.vector.tensor_tensor(out=dtv[:], in0=t_a[:], in1=t_b[:], op=ALU.add),
                     reads=[pr_r], awrites=[pr_r])
                P.op("act", lambda: nc.scalar.activation(out=eab[:], in_=alb[:], func=AF.Exp), reads=[k_r], awrites=[pr_r])
                P.op("dve", lambda: nc.vector.scalar_tensor_tensor(out=adt[:], in0=dtv[:], scalar=-1.0, in1=bc16(eab),
                                                                   op0=ALU.mult, op1=ALU.mult),
                     reads=[pr_r], awrites=[pr_r])
                adt_f = adt[:].rearrange("p c h -> p (c h)")
                cum_f = cum[:].rearrange("p c h -> p (c h)")
                tot_f = tot[:].rearrange("p c h -> p (c h)")
                for o in range(0, NC16, 512):
                    n = min(512, NC16 - o)
                    mm(pb[0][:, 0:n], UTf[:], adt_f[:, o:o + n], True, True, [pr_r, k_r], pbr[0])
                    P.op("act", lambda: nc.scalar.copy(out=cum_f[:, o:o + n], in_=pb[0][:, 0:n]), reads=[pbr[0]], awrites=[pr_r])
                    mm(pb[1][:, 0:n], ones_f[:], adt_f[:, o:o + n], True, True, [pr_r, cres], pbr[1])
                    P.op("act", lambda: nc.scalar.copy(out=tot_f[:, o:o + n], in_=pb[1][:, 0:n]), reads=[pbr[1]], awrites=[pr_r])
                P.op("act", lambda: nc.scalar.activation(out=eal[:], in_=cum[:], func=AF.Exp), reads=[pr_r], awrites=[pr_r])
                P.op("act", lambda: nc.scalar.activation(out=cdv[:], in_=tot[:], func=AF.Exp), reads=[pr_r], awrites=[pr_r])
                P.op("dve", lambda: nc.vector.tensor_tensor(out=wse[:], in0=tot[:], in1=cum[:], op=ALU.subtract),
                     reads=[pr_r], awrites=[pr_r])
                P.op("act", lambda: nc.scalar.activation(out=wse[:], in_=wse[:], func=AF.Exp), reads=[pr_r], awrites=[pr_r])
                ck(1)
                xs_t = [S3("xs_t%d" % i, [128, 1024], BF16) for i in range(2)]; xs_r = [Res(), Res()]
                z_t = [S3("z_t%d" % i, [128, 1024], BF16) for i in range(2)]; z_r = [Res(), Res()]
                b_t = [S3("b_t%d" % i, [128, 256], BF16) for i in range(2)]; b_r = [Res(), Res()]
                Rm = S3("Rm", [128, 16, 128]); Rm_r = Res()
                NM = S3("NM", [128, 16, 128]); NM_r = Res()
                seg = S3("seg", [128, 16, 128]); seg_r = Res()
                MT = S3("MT", [128, 16, 128], BF16); MT_r = Res()
                xdt = S3("xdt", [128, 16, 64], BF16); xdt_r = Res()
                xdtw = S3("xdtw", [128, 16, 64], BF16); xdtw_r = Res()
                yv = S3("yv", [128, 1024]); yv_r = Res()
                t2 = S3("t2", [128, 1024]); t2_r = Res()
                sz = S3("sz", [128, 1024]); sz_r = Res()
                junk = S3("junk", [128, 512]); junk_r = Res()
                ss = S3("ss", [128, 2]); ss_r = Res()
                sd = S3("sd", [128, 2]); rs = S3("rs", [128, 2]); rs_r = Res()
                yn = S3("yn", [128, 1024], BF16); yn_r = Res()
                ytt = S3("ytt", [128, 8, 512], BF16); ytt_r = Res()
                Sf = S3("Sf", [128, 1024]); Sf_r = Res()
                Sbf = S3("Sbf", [128, 1024], BF16); Sbf_r = Res()
                v3 = lambda t: t.rearrange("p (h d) -> p h d", d=64)
                for c in range(NCH):
                    cs = slice(c * 128, (c + 1) * 128)
                    bi = c % 2
                    ti = c // 4
                    X_, Z_, B_ = xs_t[bi], z_t[bi], b_t[bi]
                    P.dma("ld", lambda: nc.sync.dma_start(out=X_[:], in_=XSs[cs, :]), reads=[XSr[ti]], writes=[xs_r[bi]])
                    P.dma("ld", lambda: nc.sync.dma_start(out=Z_[:], in_=Zs[cs, :]), reads=[Zr[ti]], writes=[z_r[bi]])
                    P.dma("ld", lambda: nc.sync.dma_start(out=B_[:], in_=Bs[cs, :]), reads=[Br[ti]], writes=[b_r[bi]])
                    P.op("pool", lambda: nc.gpsimd.tensor_tensor(
                        out=Rm[:], in0=UTf[:, None, :].broadcast_to([128, 16, 128]),
                        in1=adt[:, c, :, None].broadcast_to([128, 16, 128]), op=ALU.mult),
                        reads=[pr_r, k_r], writes=[Rm_r])
                    P.op("pool", lambda: nc.gpsimd.tensor_tensor(
                        out=NM[:], in0=MB[:, None, :].broadcast_to([128, 16, 128]),
                        in1=cum[:, c, :, None].broadcast_to([128, 16, 128]), op=ALU.subtract),
                        reads=[pr_r, k_r], writes=[NM_r])
                    for j in range(4):
                        mm(pb[j][:], ones_f[:], Rm[:, 4 * j:4 * j + 4, :].rearrange("p h l -> p (h l)"), True, True,
                           [Rm_r, cres], pbr[j])
                        P.op("dve", lambda: nc.vector.tensor_tensor(
                            out=seg[:, 4 * j:4 * j + 4, :].rearrange("p h l -> p (h l)"), in0=pb[j][:],
                            in1=NM[:, 4 * j:4 * j + 4, :].rearrange("p h l -> p (h l)"), op=ALU.add),
                            reads=[pbr[j], NM_r], awrites=[seg_r])
                    P.op("act", lambda: nc.scalar.activation(out=seg[:], in_=seg[:], func=AF.Exp), reads=[seg_r], awrites=[seg_r])
                    ck(2)
                    for g in range(2):
                        mm(pb[4][:, g * 128:(g + 1) * 128], BT[:, g, cs], CT[:, g, cs], True, True, [bc_r], pbr[4])
                    for g in range(2):
                        P.op("dve", lambda: nc.vector.tensor_tensor(
                            out=MT[:, 8 * g:8 * g + 8, :], in0=seg[:, 8 * g:8 * g + 8, :],
                            in1=pb[4][:, None, g * 128:(g + 1) * 128].broadcast_to([128, 8, 128]), op=ALU.mult),
                            reads=[seg_r, pbr[4]], awrites=[MT_r])
                    ck(3)
                    P.op("pool", lambda: nc.gpsimd.tensor_tensor(
                        out=xdt[:], in0=v3(X_[:]), in1=dtv[:, c, :, None].broadcast_to([128, 16, 64]), op=ALU.mult),
                        reads=[xs_r[bi], pr_r], writes=[xdt_r])
                    P.op("pool", lambda: nc.gpsimd.tensor_tensor(
                        out=xdtw[:], in0=xdt[:], in1=wse[:, c, :, None].broadcast_to([128, 16, 64]), op=ALU.mult),
                        reads=[xdt_r, pr_r], writes=[xdtw_r])
                    for h in range(16):
                        bk = pb[5 - h // 8]
                        mm(bk[:, (h % 8) * 64:(h % 8 + 1) * 64], MT[:, h, :], xdt[:, h, :], True, True, [MT_r, xdt_r], pbr[5 - h // 8])
                    ck(4)
                    if c > 0:
                        for g in range(2):
                            mm(pb[g][:], CT[:, g, cs], Sbf[:, g * 512:(g + 1) * 512], True, True, [bc_r, Sbf_r], pbr[g])
                        for g in range(2):
                            P.op("dve", lambda: nc.vector.tensor_tensor(
                                out=v3(yv[:, g * 512:(g + 1) * 512]), in0=v3(pb[g][:]),
                                in1=eal[:, c, 8 * g:8 * g + 8, None].broadcast_to([128, 8, 64]), op=ALU.mult),
                                reads=[pbr[g], pr_r], awrites=[yv_r])
                            P.op("dve", lambda: nc.vector.tensor_tensor(
                                out=yv[:, g * 512:(g + 1) * 512], in0=pb[5 - g][:], in1=yv[:, g * 512:(g + 1) * 512], op=ALU.add),
                                reads=[pbr[5 - g], yv_r], awrites=[yv_r])
                    else:
                        for g in range(2):
                            P.op("dve", lambda: nc.vector.tensor_copy(out=yv[:, g * 512:(g + 1) * 512], in_=pb[5 - g][:]),
                                 reads=[pbr[5 - g]], awrites=[yv_r])
                    P.op("pool", lambda: nc.gpsimd.tensor_tensor(
                        out=v3(t2[:]), in0=v3(X_[:]), in1=dsk[:, :, None].broadcast_to([128, 16, 64]), op=ALU.mult),
                        reads=[xs_r[bi], k_r], writes=[t2_r])
                    P.op("dve", lambda: nc.vector.tensor_tensor(out=yv[:], in0=yv[:], in1=t2[:], op=ALU.add),
                         reads=[yv_r, t2_r], awrites=[yv_r])
                    P.op("act", lambda: nc.scalar.activation(out=sz[:], in_=Z_[:], func=AF.Silu), reads=[z_r[bi]], writes=[sz_r])
                    P.op("dve", lambda: nc.vector.tensor_tensor(out=yv[:], in0=yv[:], in1=sz[:], op=ALU.mult),
                         reads=[yv_r, sz_r], awrites=[yv_r])
                    for g in range(2):
                        P.op("act", lambda: nc.scalar.activation(out=junk[:], in_=yv[:, g * 512:(g + 1) * 512], func=AF.Square,
                                                                 accum_out=ss[:, g:g + 1]),
                             reads=[yv_r], writes=[junk_r], awrites=[ss_r])
                    P.op("act", lambda: nc.scalar.activation(out=sd[:], in_=ss[:], func=AF.Sqrt, bias=epsT[:, 0:1], scale=1.0 / 512),
                         reads=[ss_r, cres], awrites=[rs_r])
                    P.op("dve", lambda: nc.vector.reciprocal(out=rs[:], in_=sd[:]), reads=[rs_r], awrites=[rs_r])
                    for g in range(2):
                        P.op("dve", lambda: nc.vector.scalar_tensor_tensor(
                            out=yn[:, g * 512:(g + 1) * 512], in0=yv[:, g * 512:(g + 1) * 512], scalar=rs[:, g:g + 1],
                            in1=gsb[:, g * 512:(g + 1) * 512], op0=ALU.mult, op1=ALU.mult),
                            reads=[yv_r, rs_r, k_r], awrites=[yn_r])
                    ck(5)
                    SV = int(os.environ.get("SSD_V", "0"))
                    for half in range(2):
                        tb, tb_r = tbank()
                        for q4 in range(4):
                            kc = half * 4 + q4
                            if SV == 2:
                                continue
                            P.op("pe", lambda: nc.tensor.transpose(out=tb[:, q4 * 128:(q4 + 1) * 128], in_=yn[:, kc * 128:(kc + 1) * 128],
                                                                   identity=ident[:]),
                                 reads=[yn_r, cres], awrites=[tb_r])
                        if SV == 1:
                            continue
                        P.op("act", lambda: nc.scalar.copy(out=ytt[:, half * 4:half * 4 + 4, (c % 4) * 128:(c % 4 + 1) * 128],
                                                           in_=tb.rearrange("p (k t) -> p k t", t=128)),
                             reads=[tb_r], awrites=[ytt_r])
                    if c % 4 == 3:
                        P.dma("st", lambda: nc.gpsimd.dma_start(
                            out=YTs[0:1024, ti * 512:(ti + 1) * 512].rearrange("(k p) t -> p k t", p=128), in_=ytt[:]),
                            reads=[ytt_r], awrites=[YTr[ti]])
                    ck(6)
                    if c < NCH - 1:
                        for g in range(2):
                            mm(pb[2 + g][:], B_[:, g * 128:(g + 1) * 128], xdtw[:, 8 * g:8 * g + 8, :].rearrange("p h d -> p (h d)"),
                               True, True, [b_r[bi], xdtw_r], pbr[2 + g])
                        if c > 0:
                            P.op("dve", lambda: nc.vector.tensor_tensor(
                                out=v3(Sf[:]), in0=v3(Sf[:]), in1=cdv[:, c, :, None].broadcast_to([128, 16, 64]), op=ALU.mult),
                                reads=[Sf_r, pr_r], awrites=[Sf_r])
                            for g in range(2):
                                P.op("dve", lambda: nc.vector.tensor_tensor(
                                    out=Sf[:, g * 512:(g + 1) * 512], in0=pb[2 + g][:], in1=Sf[:, g * 512:(g + 1) * 512], op=ALU.add),
                                    reads=[pbr[2 + g], Sf_r], awrites=[Sf_r])
                        else:
                            for g in range(2):
                                P.op("dve", lambda: nc.vector.tensor_copy(out=Sf[:, g * 512:(g + 1) * 512], in_=pb[2 + g][:]),
                                     reads=[pbr[2 + g]], awrites=[Sf_r])
                        P.op("act", lambda: nc.scalar.copy(out=Sbf[:], in_=Sf[:]), reads=[Sf_r], writes=[Sbf_r])
                barrier()

        barrier()
        for l in range(NL):
            phase_in(l)
            if stop_after == "in":
                break
            phase_ssd(l)
            if stop_after == "ssd":
                break
            if stop_after != "noatt":
                phase_att(l)
            if stop_after == "att":
                break
            phase_out(l)

        P.finish()
        sems = {}
        for e in COMPUTE:
            sems[e] = stack.enter_context(nc.semaphore("s_" + e))
        for q, Q in P.dmaq.items():
            for s in range(Q["k"]):
                sems[(q, s)] = stack.enter_context(nc.semaphore("s_%s%d" % (q, s)))
        P.emit(sems)
    return real_nc


_NC_CACHE = {}
_PNAMES = ["w_in", "w_out", "w_up", "w_down", "g_mix", "g_mlp", "conv_w", "conv_b", "dt_bias", "a_log",
           "d_skip", "g_ssd", "g_q", "g_k", "g_kidx"]


def kernel(**inputs):
    x = np.asarray(inputs["x"], dtype=np.float32)
    Bn, L, _ = x.shape
    NL = int(np.asarray(inputs["w_in"]).shape[0])
    key = (L, NL)
    if key not in _NC_CACHE:
        _NC_CACHE[key] = build(L=L, NL=NL)
    nc = _NC_CACHE[key]
    common = {n: np.ascontiguousarray(np.asarray(inputs[n], dtype=np.float32)) for n in _PNAMES}
    in_maps = [dict(common, xT=np.ascontiguousarray(x[b].T)) for b in range(Bn)]
    res = run_bass_kernel_spmd(nc, in_maps, core_ids=list(range(Bn)))
    out = np.stack([np.asarray(res.results[b]["outT"]).T for b in range(Bn)], axis=0)
    return np.ascontiguousarray(out.astype(np.float32))
```

```python
import os
import numpy as np
import concourse.bass as bass
import concourse.mybir as mybir
from concourse.bass_utils import run_bass_kernel_spmd

F32 = mybir.dt.float32
BF16 = mybir.dt.bfloat16
AF = mybir.ActivationFunctionType
ALU = mybir.AluOpType
AX = mybir.AxisListType

D = 2048
DFF = 8192
EPS = 1e-6
NEG = -1.0e30
COMPUTE = ("pe", "act", "dve", "pool")


class Res:
    __slots__ = ("w", "r", "pr", "name")

    def __init__(self, name=""):
        self.w = {}
        self.r = {}
        self.pr = {}
        self.name = name


def _merge(a, b):
    o = dict(a)
    for k, v in b.items():
        if o.get(k, 0) < v:
            o[k] = v
    return o


_REC = []


class _EngProxy:
    def __init__(self, eng):
        self._eng = eng

    def __getattr__(self, name):
        real = getattr(self._eng, name)
        if not callable(real):
            return real

        def f(*a, **k):
            _REC.append((real, a, k))
            return None

        return f


class _NcProxy:
    def __init__(self, nc):
        self._nc = nc
        for n in ("tensor", "vector", "scalar", "gpsimd", "sync"):
            setattr(self, n, _EngProxy(getattr(nc, n)))

    def __getattr__(self, name):
        return getattr(self._nc, name)


def _record(fn):
    del _REC[:]
    fn()
    calls = list(_REC)
    del _REC[:]
    assert calls

    def run():
        ins = None
        for real, a, k in calls:
            ins = real(*a, **k)
        return ins

    return run


class Prog:
    def __init__(self, nc, nld=8, nst=8):
        self.nc = nc
        self.streams = {h: [] for h in ("pe", "act", "dve", "pool", "sp")}
        self.nops = {e: 0 for e in COMPUTE}
        self.needed = {e: set() for e in COMPUTE}
        self.waited = {h: {} for h in self.streams}
        self.dmaq = {
            "ld": dict(host="sp", k=nld, n=0, cnt=[0] * nld),
            "st": dict(host="pool", k=nst, n=0, cnt=[0] * nst),
            "cv": dict(host="pool", k=5, n=0, cnt=[0] * 5),
        }

    def _deps(self, eng, reads, writes, awrites):
        deps = {}

        def add(k, v, kind):
            if k == eng and eng == "pe":
                return
            if deps.get(k, 0) < v:
                deps[k] = v

        for r in reads:
            for k, v in r.w.items():
                add(k, v, "raw")
        for w in writes:
            w.pr = _merge(w.w, w.r)
            w.w = {}
            w.r = {}
            for k, v in w.pr.items():
                add(k, v, "war")
        for w in awrites:
            if w.r:
                w.pr = _merge(w.w, w.r)
                w.w = {}
                w.r = {}
            for k, v in w.pr.items():
                add(k, v, "war")
        return deps

    def _emit(self, host, deps, fn, inc):
        waits = []
        wt = self.waited[host]
        for k, v in deps.items():
            if wt.get(k, 0) >= v:
                continue
            wt[k] = v
            waits.append((k, v))
            if k in COMPUTE:
                self.needed[k].add(v)
        self.streams[host].append((waits, fn, inc))

    def op(self, eng, fn, reads=(), writes=(), awrites=()):
        deps = self._deps(eng, reads, writes, awrites)
        self.nops[eng] += 1
        idx = self.nops[eng]
        self._emit(eng, deps, _record(fn), ("c", eng, idx))
        for r in reads:
            if r.r.get(eng, 0) < idx:
                r.r[eng] = idx
        for w in list(writes) + list(awrites):
            if w.w.get(eng, 0) < idx:
                w.w[eng] = idx

    def dma(self, q, fn, reads=(), writes=(), awrites=()):
        Q = self.dmaq[q]
        slot = Q["n"] % Q["k"]
        Q["n"] += 1
        deps = self._deps(q, reads, writes, awrites)
        key = (q, slot)
        if Q["cnt"][slot] > 0:
            deps[key] = max(deps.get(key, 0), Q["cnt"][slot] * 16)
        Q["cnt"][slot] += 1
        val = Q["cnt"][slot] * 16
        self._emit(Q["host"], deps, _record(fn), ("d", q, slot))
        for r in reads:
            if r.r.get(key, 0) < val:
                r.r[key] = val
        for w in list(writes) + list(awrites):
            if w.w.get(key, 0) < val:
                w.w[key] = val

    def finish(self):
        deps = {}
        for q, Q in self.dmaq.items():
            for s in range(Q["k"]):
                if Q["cnt"][s]:
                    deps[(q, s)] = Q["cnt"][s] * 16
        for e in COMPUTE:
            if self.nops[e]:
                deps[e] = self.nops[e]
        self._emit("sp", deps, None, None)

    def emit(self, sems):
        nc = self.nc
        rank = {}
        for e in COMPUTE:
            rank[e] = {v: i + 1 for i, v in enumerate(sorted(self.needed[e]))}
        engs = {"pe": "tensor", "act": "scalar", "dve": "vector", "pool": "gpsimd", "sp": "sync"}
        with nc.Block() as block:
            for host, attr in engs.items():
                stream = self.streams[host]

                def body(eng, stream=stream, host=host):
                    for waits, fn, inc in stream:
                        for k, v in waits:
                            if k in COMPUTE:
                                eng.wait_ge(sems[k], rank[k][v])
                            else:
                                eng.wait_ge(sems[k], v)
                        if fn is None:
                            continue
                        ins = fn()
                        if inc[0] == "c":
                            if inc[2] in self.needed[inc[1]]:
                                ins.then_inc(sems[inc[1]], 1)
                        else:
                            ins.then_inc(sems[(inc[1], inc[2])], 16)

                getattr(block, attr)(body)


N_Z, N_XBC, N_DT, N_Q, N_KV, N_QI, N_KI, N_WI = 1024, 1536, 16, 1024, 512, 1024, 64, 16
O_Z = 0
O_XBC = O_Z + N_Z
O_DT = O_XBC + N_XBC
O_Q = O_DT + N_DT
O_K = O_Q + N_Q
O_V = O_K + N_KV
O_QI = O_V + N_KV
O_KI = O_QI + N_QI
O_WI = O_KI + N_KI
IN_W = O_WI + N_WI

INBLK = []
for i in range(2):
    INBLK.append(("z%d" % i, [(O_Z + 512 * i, 512, 0)]))
for i in range(3):
    INBLK.append(("xbc%d" % i, [(O_XBC + 512 * i, 512, 0)]))
for i in range(2):
    INBLK.append(("q%d" % i, [(O_Q + 512 * i, 512, 0)]))
INBLK.append(("k0", [(O_K, 512, 0)]))
INBLK.append(("v0", [(O_V, 512, 0)]))
for i in range(2):
    INBLK.append(("qi%d" % i, [(O_QI + 512 * i, 512, 0)]))
INBLK.append(("misc", [(O_KI, 64, 0), (O_KI, 64, 64), (O_DT, 16, 128), (O_WI, 16, 144), (O_Z, 352, 160)]))
NB_IN = len(INBLK)
NB_OUT = 4
NB_UP = 16
NB_DN = 16
NB_LAYER = NB_IN + NB_OUT + NB_UP + NB_DN
B_IN, B_OUT, B_UP, B_DN = 0, NB_IN, NB_IN + NB_OUT, NB_IN + NB_OUT + NB_UP


def build(L=4096, NL=2, dbg=False, stop_after=None):
    nc = bass.Bass("TRN2", target_bir_lowering=False)
    TT = 512
    NT = L // TT
    NCH = L // 128
    ikind = "ExternalOutput" if dbg else "Internal"

    def din(name, shape, dt=F32):
        return nc.dram_tensor(name, list(shape), dt, kind="ExternalInput").ap()

    def dscr(name, shape, dt=BF16):
        return nc.dram_tensor(name, list(shape), dt, kind=ikind).ap()

    xT_in = din("xT", [D, L])
    w_in = din("w_in", [NL, D, IN_W])
    w_out = din("w_out", [NL, D, D])
    w_up = din("w_up", [NL, D, DFF])
    w_down = din("w_down", [NL, DFF, D])
    g_mix = din("g_mix", [NL, D])
    g_mlp = din("g_mlp", [NL, D])
    conv_w = din("conv_w", [NL, 4, 1536])
    conv_b = din("conv_b", [NL, 1536])
    dt_bias = din("dt_bias", [NL, 16])
    a_log = din("a_log", [NL, 16])
    d_skip = din("d_skip", [NL, 16])
    g_ssd = din("g_ssd", [NL, 1024])
    g_q = din("g_q", [NL, 128])
    g_k = din("g_k", [NL, 128])
    g_kidx = din("g_kidx", [NL, 64])
    outT = nc.dram_tensor("outT", [D, L], F32, kind="ExternalOutput").ap()

    WB = dscr("WB", [NL * NB_LAYER, 128, 16 * 512])
    XT = dscr("XTs", [D, L], F32)
    Zs = dscr("Zs", [L, 1024])
    Vs = dscr("Vs", [L, 512])
    DW = dscr("DWs", [L, 32], F32)
    XSs = dscr("XSs", [L, 1024])
    Bs = dscr("Bs", [L, 256])
    BCT = dscr("BCT", [512, L])
    QTs = dscr("QTs", [1024, L])
    KTs = dscr("KTs", [512, L])
    QITs = dscr("QITs", [1024, L])
    KITs = dscr("KITs", [128, L])
    YTs = dscr("YTs", [D, L])

    P = Prog(nc)
    real_nc = nc
    nc = _NcProxy(real_nc)
    sb = {}

    import contextlib

    stack = contextlib.ExitStack()

    def S(name, shape, dt=F32):
        t = stack.enter_context(nc.sbuf_tensor(name, list(shape), dt))
        return t

    def PS(name, shape, dt=F32):
        return stack.enter_context(nc.psum_tensor(name, list(shape), dt))

    with stack:
        pb = [PS("pb%d" % i, [128, 512], F32) for i in range(6)]
        pbr = [Res("pb%d" % i) for i in range(6)]
        ptrs = [PS("ptr%d" % i, [128, 1024], BF16) for i in range(2)]

        ident_f = S("ident_f", [128, 128], F32)
        ident = S("ident", [128, 128], BF16)
        ones_bf = S("ones_bf", [128, 128], BF16)
        blk_bf = S("blk_bf", [128, 128], BF16)
        ones_f = S("ones_f", [128, 128], F32)
        epsT = S("epsT", [128, 1], F32)
        cres = Res("consts")

        P.op("pool", lambda: nc.gpsimd.memset(ones_f[:], 1.0), awrites=[cres])
        P.op("pool", lambda: nc.gpsimd.memset(ones_bf[:], 1.0), awrites=[cres])
        P.op("pool", lambda: nc.gpsimd.memset(epsT[:], EPS), awrites=[cres])
        blk_r = Res("blk")
        P.op("pool", lambda: nc.gpsimd.memset(blk_bf[:], 0.0), writes=[blk_r])
        P.op("pool", lambda: nc.gpsimd.memset(blk_bf[0:64, 0:64], 1.0), writes=[blk_r])
        P.op("pool", lambda: nc.gpsimd.memset(blk_bf[64:128, 64:128], 1.0), writes=[blk_r])
        idf_r = Res("idf")
        P.op("pool", lambda: nc.gpsimd.affine_select(out=ident_f[:], in_=ones_f[:], pattern=[[-1, 128]],
                                                     compare_op=ALU.is_equal, fill=0.0, base=0, channel_multiplier=1),
             reads=[cres], writes=[idf_r])
        P.op("pool", lambda: nc.gpsimd.tensor_copy(out=ident[:], in_=ident_f[:]), reads=[idf_r, blk_r], awrites=[cres])

        gmixT = S("gmixT", [128, NL, 16], F32)
        gmlpT = S("gmlpT", [128, NL, 16], F32)
        cwT = S("cwT", [128, NL, 12, 4], F32)
        cbT = S("cbT", [128, NL, 12], F32)
        gqT = S("gqT", [128, NL], F32)
        gkT = S("gkT", [128, NL], F32)
        gkiT = S("gkiT", [128, NL], F32)
        pres = Res("params")
        with nc.allow_non_contiguous_dma(reason="tiny param loads"):
            pass

        def ldp(out, in_):
            P.dma("ld", lambda: nc.sync.dma_start(out=out, in_=in_, allow_slow_non_contiguous=True),
                  awrites=[pres])

        for l in range(NL):
            ldp(gmixT[:, l, :], g_mix[l].rearrange("(k p) -> p k", p=128))
            ldp(gmlpT[:, l, :], g_mlp[l].rearrange("(k p) -> p k", p=128))
            for j in range(4):
                ldp(cwT[:, l, :, j], conv_w[l, j].rearrange("(c p) -> p c", p=128))
            ldp(cbT[:, l, :], conv_b[l].rearrange("(c p) -> p c", p=128))
            ldp(gqT[:, l:l + 1], g_q[l].rearrange("(p o) -> p o", o=1))
            ldp(gkT[:, l:l + 1], g_k[l].rearrange("(p o) -> p o", o=1))
            ldp(gkiT[0:64, l:l + 1], g_kidx[l].rearrange("(p o) -> p o", o=1))
            ldp(gkiT[64:128, l:l + 1], g_kidx[l].rearrange("(p o) -> p o", o=1))
        P.op("dve", lambda: nc.vector.tensor_scalar(out=gqT[:], in0=gqT[:], scalar1=float(128 ** -0.5),
                                                    scalar2=None, op0=ALU.mult),
             reads=[pres], awrites=[pres])

        pst = contextlib.ExitStack()
        S0 = S
        WBr = [Res("WB%d" % i) for i in range(NL * NB_LAYER)]
        castq = []
        for l in range(NL):
            base = l * NB_LAYER
            for bi, (nm, segs) in enumerate(INBLK):
                pcs = [(w_in[l, :, so:so + n].rearrange("(k p) j -> p k j", p=128), dc, n) for so, n, dc in segs]
                castq.append((base + B_IN + bi, pcs))
            for bi in range(NB_OUT):
                castq.append((base + B_OUT + bi,
                              [(w_out[l, :, bi * 512:(bi + 1) * 512].rearrange("(k p) j -> p k j", p=128), 0, 512)]))
            for qd in range(4):
                for bi in range(4):
                    castq.append((base + B_UP + qd * 4 + bi,
                                  [(w_up[l, :, (qd * 4 + bi) * 512:(qd * 4 + bi + 1) * 512].rearrange("(k p) j -> p k j", p=128), 0, 512)]))
                for mw in range(4):
                    castq.append((base + B_DN + qd * 4 + mw,
                                  [(w_down[l, qd * 2048:(qd + 1) * 2048, mw * 512:(mw + 1) * 512]
                                    .rearrange("(k p) j -> p k j", p=128), 0, 512)]))
        castpos = [0]
        cast_done = set()

        def pump(n):
            for _ in range(n):
                if castpos[0] >= len(castq):
                    return
                bidx, pcs = castq[castpos[0]]
                castpos[0] += 1
                cast_done.add(bidx)
                for src, dc, nn in pcs:
                    P.dma("cv", lambda: nc.gpsimd.dma_start(
                        out=WB[bidx].rearrange("p (k j) -> p k j", j=512)[:, :, dc:dc + nn], in_=src,
                        allow_slow_non_contiguous=(nn < 128)), awrites=[WBr[bidx]])

        pump(3)
        pst.close()
        S = S0

        def barrier():
            deps = {}
            for e in COMPUTE:
                if P.nops[e]:
                    deps[e] = P.nops[e]
            for q, Q in P.dmaq.items():
                for s_ in range(Q["k"]):
                    if Q["cnt"][s_]:
                        deps[(q, s_)] = Q["cnt"][s_] * 16
            for host in P.streams:
                dd = {k: v for k, v in deps.items() if k != host}
                P._emit(host, dd, None, None)

        wbuf = [None] * 4
        wbuf_r = [Res("wbuf%d" % i) for i in range(4)]
        wlimit = [0]

        def walloc(stk, tag, count):
            for i in range(4):
                wbuf[i] = stk.enter_context(nc.sbuf_tensor("wbuf%d_%s" % (i, tag), [128, 16, 512], BF16))
            assert wpos[0] == wpos[1]
            wlimit[0] = wpos[0] + count
        plan = []
        for l in range(NL):
            base = l * NB_LAYER
            for ti in range(NT):
                plan += [base + B_IN + b for b in range(NB_IN)]
            for ti in range(NT):
                plan += [base + B_OUT + b for b in range(NB_OUT)]
                for qd in range(4):
                    plan += [base + B_UP + qd * 4 + b for b in range(4)]
                    plan += [base + B_DN + qd * 4 + b for b in range(4)]
        wpos = [0, 0]

        def wissue():
            n = wpos[1]
            if n >= wlimit[0]:
                return
            wpos[1] += 1
            sl = n % 4
            b = plan[n]
            while b not in cast_done:
                pump(1)
            P.dma("ld", lambda: nc.sync.dma_start(out=wbuf[sl][:], in_=WB[b].rearrange("p (k j) -> p k j", j=512)),
                  reads=[WBr[b]], writes=[wbuf_r[sl]])

        def wget(b):
            n = wpos[0]
            assert plan[n] == b, (n, plan[n], b)
            while wpos[1] < min(n + 3, wlimit[0]):
                wissue()
            wpos[0] += 1
            return wbuf[n % 4], wbuf_r[n % 4]

        NTr = lambda nm: [Res("%s%d" % (nm, i)) for i in range(NT)]
        XTr, Zr, Vr, DWr, XSr, Br, BCTr, QTr, KTr, QITr, KITr, YTr = [NTr(n) for n in
            ("XT", "Z", "V", "DW", "XS", "B", "BCT", "QT", "KT", "QIT", "KIT", "YT")]
        bankrot = [0]

        def nbank():
            i = bankrot[0] % 4
            bankrot[0] += 1
            return pb[i], pbr[i]

        nrot = [0]

        def nnbank():
            i = 4 + nrot[0] % 2
            nrot[0] += 1
            return pb[i], pbr[i]

        trot = [0]

        tb_only0 = [False]

        def tbank():
            i = 0 if tb_only0[0] else trot[0] % 2
            trot[0] += 1
            return ptrs[i][:, 0:512], ptr_rr[i]

        ptr_rr = [Res("ptr0"), Res("ptr1")]

        def mm(out, lhsT, rhs, start, stop, reads, bank_r):
            P.op("pe", lambda: nc.tensor.matmul(out, lhsT, rhs, start=start, stop=stop), reads=reads, awrites=[bank_r])

        def phase_in(l):
            base = l * NB_LAYER
            xsrc = xT_in if l == 0 else XT
            st2 = contextlib.ExitStack()

            def S2(name, shape, dt=F32):
                return st2.enter_context(nc.sbuf_tensor("%s_%d" % (name, l), list(shape), dt))

            with st2:
                walloc(st2, "in%d" % l, NT * NB_IN)
                xin = S2("xin", [128, 16, TT]); xin_r = Res()
                xn = [S2("xn%d" % i, [128, 16, TT], BF16) for i in range(2)]; xn_r = [Res(), Res()]
                sqb = [S2("sqb%d" % i, [128, TT], BF16) for i in range(2)]; sqb_r = [Res(), Res()]
                std = S2("std", [128, TT]); std_r = Res()
                rstd = S2("rstd", [128, TT]); rstd_r = Res()
                U = [S2("U%d" % i, [128, TT + 3]) for i in range(2)]; U_r = [Res(), Res()]
                acc = [S2("acc%d" % i, [128, TT]) for i in range(2)]; acc_r = [Res(), Res()]
                sbf = [S2("sbf%d" % i, [128, TT], BF16) for i in range(2)]; sbf_r = [Res(), Res()]
                qst = [S2("qst%d" % i, [128, TT], BF16) for i in range(2)]; qst_r = [Res(), Res()]
                carry = S2("carry", [128, 12, 3]); carry_r = [Res() for _ in range(12)]
                XStile = S2("XStile", [128, 4, 1024], BF16); XStile_r = Res()
                Btile = S2("Btile", [128, 4, 256], BF16); Btile_r = Res()
                Ztile = S2("Ztile", [128, 4, 1024], BF16); Ztile_r = Res()
                Vtile = S2("Vtile", [128, 4, 512], BF16); Vtile_r = Res()
                DWtile = S2("DWtile", [128, 4, 32]); DWtile_r = Res()
                cnt = {"sq": 0, "u": 0, "q": 0}
                pend = []

                def colsum_rstd(src_ap, src_r, lhs_ones, scale):
                    i = cnt["sq"] % 2; cnt["sq"] += 1
                    P.op("act", lambda: nc.scalar.activation(out=sqb[i][:], in_=src_ap, func=AF.Square),
                         reads=[src_r], writes=[sqb_r[i]])
                    nb, nb_r = nnbank()
                    mm(nb[:], lhs_ones[:], sqb[i][:], True, True, [sqb_r[i], cres], nb_r)
                    P.op("act", lambda: nc.scalar.activation(out=std[:], in_=nb[:], func=AF.Sqrt,
                                                             bias=epsT[:, 0:1], scale=scale),
                         reads=[nb_r, cres], writes=[std_r])
                    P.op("dve", lambda: nc.vector.reciprocal(out=rstd[:], in_=std[:]), reads=[std_r], writes=[rstd_r])

                def norm_tile(ti):
                    t0 = ti * TT
                    xr = [] if l == 0 else [XTr[ti]]
                    P.dma("ld", lambda: nc.sync.dma_start(
                        out=xin[:], in_=xsrc[:, t0:t0 + TT].rearrange("(k p) t -> p k t", p=128)),
                        reads=xr, writes=[xin_r])
                    nb, nb_r = nnbank()
                    for kc in range(16):
                        i = cnt["sq"] % 2; cnt["sq"] += 1
                        P.op("act", lambda kc=kc, i=i: nc.scalar.activation(out=sqb[i][:], in_=xin[:, kc, :], func=AF.Square),
                             reads=[xin_r], writes=[sqb_r[i]])
                        mm(nb[:], ones_bf[:], sqb[i][:], kc == 0, kc == 15, [sqb_r[i], cres], nb_r)
                    P.op("act", lambda: nc.scalar.activation(out=std[:], in_=nb[:], func=AF.Sqrt,
                                                             bias=epsT[:, 0:1], scale=1.0 / D),
                         reads=[nb_r, cres], writes=[std_r])
                    P.op("dve", lambda: nc.vector.reciprocal(out=rstd[:], in_=std[:]), reads=[std_r], writes=[rstd_r])
                    Xn = xn[ti % 2]; Xn_r = xn_r[ti % 2]
                    for kc in range(16):
                        P.op("dve", lambda kc=kc: nc.vector.scalar_tensor_tensor(
                            out=Xn[:, kc, :], in0=xin[:, kc, :], scalar=gmixT[:, l, kc:kc + 1], in1=rstd[:],
                            op0=ALU.mult, op1=ALU.mult),
                            reads=[xin_r, rstd_r, pres], awrites=[Xn_r])

                norm_tile(0)
                for ti in range(NT):
                    t0 = ti * TT
                    pump(1)
                    X = xn[ti % 2]; X_r = xn_r[ti % 2]

                    def flush():
                        fs = list(pend)
                        del pend[:]
                        for f in fs:
                            f()

                    def fm_chunk(wb, wb_r, sub):
                        bk, bk_r = nbank()
                        for kc in range(16):
                            mm(bk[:], wb[:, kc, sub * 128:(sub + 1) * 128], X[:, kc, :], kc == 0, kc == 15, [wb_r, X_r], bk_r)
                        flush()
                        return bk, bk_r

                    def tm_chunk(wb, wb_r, tsub, c0, n):
                        bk, bk_r = nbank()
                        for kc in range(16):
                            mm(bk[:, 0:n], X[:, kc, tsub * 128:(tsub + 1) * 128], wb[:, kc, c0:c0 + n], kc == 0, kc == 15,
                               [wb_r, X_r], bk_r)
                        flush()
                        return bk, bk_r

                    def transposes(src, src_r, dst_view, dst_r):
                        tb, tb_r = tbank()
                        for tsub in range(4):
                            P.op("pe", lambda tsub=tsub: nc.tensor.transpose(
                                out=tb[:, tsub * 128:(tsub + 1) * 128], in_=src[:, tsub * 128:(tsub + 1) * 128], identity=ident[:]),
                                reads=[src_r, cres], awrites=[tb_r])
                        P.op("act", lambda: nc.scalar.copy(out=dst_view, in_=tb.rearrange("p (s c) -> p s c", c=128)),
                             reads=[tb_r], awrites=[dst_r])

                    for zb in range(2):
                        wb, wb_r = wget(base + B_IN + zb)
                        for tsub in range(4):
                            bk, bk_r = tm_chunk(wb, wb_r, tsub, 0, 512)
                            P.op("act", lambda bk=bk, tsub=tsub, zb=zb: nc.scalar.copy(
                                out=Ztile[:, tsub, zb * 512:(zb + 1) * 512], in_=bk[:]),
                                reads=[bk_r], awrites=[Ztile_r])
                    P.dma("st", lambda: nc.gpsimd.dma_start(
                        out=Zs[t0:t0 + TT, :].rearrange("(s p) c -> p s c", p=128), in_=Ztile[:]),
                        reads=[Ztile_r], writes=[Zr[ti]])
                    if ti + 1 < NT:
                        norm_tile(ti + 1)
                    for xb in range(3):
                        wb, wb_r = wget(base + B_IN + 2 + xb)
                        for sub in range(4):
                            cc = xb * 4 + sub
                            bk, bk_r = fm_chunk(wb, wb_r, sub)
                            ui = cnt["u"] % 2; cnt["u"] += 1
                            u, u_r, ac, ac_r, sf, sf_r = U[ui], U_r[ui], acc[ui], acc_r[ui], sbf[ui], sbf_r[ui]
                            if ti == 0:
                                P.op("dve", lambda u=u: nc.vector.memset(u[:, 0:3], 0.0), writes=[u_r])
                            else:
                                P.op("dve", lambda u=u, cc=cc: nc.vector.tensor_copy(out=u[:, 0:3], in_=carry[:, cc, :]),
                                     reads=[carry_r[cc]], writes=[u_r])
                            P.op("act", lambda u=u, bk=bk: nc.scalar.copy(out=u[:, 3:3 + TT], in_=bk[:]),
                                 reads=[bk_r], awrites=[u_r])
                            P.op("dve", lambda u=u, ac=ac, cc=cc: nc.vector.tensor_scalar(
                                out=ac[:], in0=u[:, 0:TT], scalar1=cwT[:, l, cc, 0:1], scalar2=cbT[:, l, cc:cc + 1],
                                op0=ALU.mult, op1=ALU.add), reads=[u_r, pres], writes=[ac_r])
                            for j in range(1, 4):
                                P.op("dve", lambda u=u, ac=ac, cc=cc, j=j: nc.vector.scalar_tensor_tensor(
                                    out=ac[:], in0=u[:, j:j + TT], scalar=cwT[:, l, cc, j:j + 1], in1=ac[:],
                                    op0=ALU.mult, op1=ALU.add), reads=[u_r, ac_r, pres], awrites=[ac_r])
                            P.op("dve", lambda u=u, cc=cc: nc.vector.tensor_copy(out=carry[:, cc, :], in_=u[:, TT:TT + 3]),
                                 reads=[u_r], writes=[carry_r[cc]])
                            P.op("act", lambda ac=ac, sf=sf: nc.scalar.activation(out=sf[:], in_=ac[:], func=AF.Silu),
                                 reads=[ac_r], writes=[sf_r])
                            def tail(cc=cc, sf=sf, sf_r=sf_r, ti=ti, t0=t0):
                                if cc < 8:
                                    transposes(sf, sf_r, XStile[:, :, cc * 128:(cc + 1) * 128], XStile_r)
                                elif cc < 10:
                                    transposes(sf, sf_r, Btile[:, :, (cc - 8) * 128:(cc - 7) * 128], Btile_r)
                                if cc >= 8:
                                    P.dma("st", lambda: nc.gpsimd.dma_start(
                                        out=BCT[(cc - 8) * 128:(cc - 7) * 128, t0:t0 + TT], in_=sf[:]),
                                        reads=[sf_r], awrites=[BCTr[ti]])
                                if cc == 7:
                                    P.dma("st", lambda: nc.gpsimd.dma_start(
                                        out=XSs[t0:t0 + TT, :].rearrange("(s p) c -> p s c", p=128), in_=XStile[:]),
                                        reads=[XStile_r], writes=[XSr[ti]])
                                if cc == 9:
                                    P.dma("st", lambda: nc.gpsimd.dma_start(
                                        out=Bs[t0:t0 + TT, :].rearrange("(s p) c -> p s c", p=128), in_=Btile[:]),
                                        reads=[Btile_r], writes=[Br[ti]])

                            pend.append(tail)

                    def qk_chunk(wb, wb_r, sub, gT, lhs_ones, scale, dst, dst_r):
                        bk, bk_r = fm_chunk(wb, wb_r, sub)
                        i = cnt["sq"] % 2; cnt["sq"] += 1
                        P.op("act", lambda: nc.scalar.activation(out=sqb[i][:], in_=bk[:], func=AF.Square),
                             reads=[bk_r], writes=[sqb_r[i]])

                        def tail():
                            nb, nb_r = nnbank()
                            mm(nb[:], lhs_ones[:], sqb[i][:], True, True, [sqb_r[i], cres], nb_r)
                            P.op("act", lambda: nc.scalar.activation(out=std[:], in_=nb[:], func=AF.Sqrt,
                                                                     bias=epsT[:, 0:1], scale=scale),
                                 reads=[nb_r, cres], writes=[std_r])
                            P.op("dve", lambda: nc.vector.reciprocal(out=rstd[:], in_=std[:]), reads=[std_r], writes=[rstd_r])
                            qi_ = cnt["q"] % 2; cnt["q"] += 1
                            P.op("dve", lambda: nc.vector.scalar_tensor_tensor(
                                out=qst[qi_][:], in0=bk[:], scalar=gT[:, l:l + 1], in1=rstd[:], op0=ALU.mult, op1=ALU.mult),
                                reads=[bk_r, rstd_r, pres], writes=[qst_r[qi_]])
                            P.dma("st", lambda: nc.gpsimd.dma_start(out=dst, in_=qst[qi_][:]),
                                  reads=[qst_r[qi_]], awrites=[dst_r])

                        pend.append(tail)

                    for qb in range(2):
                        wb, wb_r = wget(base + B_IN + 5 + qb)
                        for sub in range(4):
                            hh = qb * 4 + sub
                            qk_chunk(wb, wb_r, sub, gqT, ones_bf, 1.0 / 128, QTs[hh * 128:(hh + 1) * 128, t0:t0 + TT], QTr[ti])
                    wb, wb_r = wget(base + B_IN + 7)
                    for sub in range(4):
                        qk_chunk(wb, wb_r, sub, gkT, ones_bf, 1.0 / 128, KTs[sub * 128:(sub + 1) * 128, t0:t0 + TT], KTr[ti])
                    wb, wb_r = wget(base + B_IN + 8)
                    for tsub in range(4):
                        bk, bk_r = tm_chunk(wb, wb_r, tsub, 0, 512)
                        P.op("act", lambda bk=bk, tsub=tsub: nc.scalar.copy(out=Vtile[:, tsub, :], in_=bk[:]),
                             reads=[bk_r], awrites=[Vtile_r])
                    P.dma("st", lambda: nc.gpsimd.dma_start(
                        out=Vs[t0:t0 + TT, :].rearrange("(s p) c -> p s c", p=128), in_=Vtile[:]),
                        reads=[Vtile_r], writes=[Vr[ti]])
                    for qb in range(2):
                        wb, wb_r = wget(base + B_IN + 9 + qb)
                        for sub in range(4):
                            hh = qb * 4 + sub
                            bk, bk_r = fm_chunk(wb, wb_r, sub)
                            qi_ = cnt["q"] % 2; cnt["q"] += 1
                            P.op("act", lambda bk=bk, qi_=qi_: nc.scalar.mul(out=qst[qi_][:], in_=bk[:], mul=1.0 / 32),
                                 reads=[bk_r], writes=[qst_r[qi_]])
                            P.dma("st", lambda qi_=qi_, hh=hh: nc.gpsimd.dma_start(
                                out=QITs[hh * 128:(hh + 1) * 128, t0:t0 + TT], in_=qst[qi_][:]),
                                reads=[qst_r[qi_]], awrites=[QITr[ti]])
                    wb, wb_r = wget(base + B_IN + 11)
                    qk_chunk(wb, wb_r, 0, gkiT, blk_bf, 1.0 / 64, KITs[:, t0:t0 + TT], KITr[ti])
                    for tsub in range(4):
                        bk, bk_r = tm_chunk(wb, wb_r, tsub, 128, 32)
                        P.op("act", lambda bk=bk, tsub=tsub: nc.scalar.copy(out=DWtile[:, tsub, :], in_=bk[:, 0:32]),
                             reads=[bk_r], awrites=[DWtile_r])
                    P.dma("st", lambda: nc.gpsimd.dma_start(
                        out=DW[t0:t0 + TT, :].rearrange("(s p) c -> p s c", p=128), in_=DWtile[:]),
                        reads=[DWtile_r], writes=[DWr[ti]])
                    flush()
                barrier()


        def phase_out(l):
            base = l * NB_LAYER
            xsrc = xT_in if l == 0 else XT
            xdst = outT if l == NL - 1 else XT
            st4 = contextlib.ExitStack()

            def S4(name, shape, dt=F32):
                return st4.enter_context(nc.sbuf_tensor("%s_o%d" % (name, l), list(shape), dt))

            with st4:
                walloc(st4, "out%d" % l, NT * (NB_OUT + NB_UP + NB_DN))
                yt = S4("yt", [128, 16, TT], BF16); yt_r = Res()
                x1 = S4("x1", [128, 16, TT]); x1_r = Res()
                xn2 = S4("xn2", [128, 16, TT], BF16); xn2_r = Res()
                actT = [S4("actT%d" % i, [128, 16, TT], BF16) for i in range(2)]; actT_r = [Res(), Res()]
                rl = [S4("rl%d" % i, [128, TT]) for i in range(2)]; rl_r = [Res(), Res()]
                sqb = [S4("sqb%d" % i, [128, TT], BF16) for i in range(2)]; sqb_r = [Res(), Res()]
                std = S4("std", [128, TT]); std_r = Res()
                rstd = S4("rstd", [128, TT]); rstd_r = Res()
                c2 = {"sq": 0, "rl": 0}
                for ti in range(NT):
                    t0 = ti * TT
                    P.dma("ld", lambda: nc.sync.dma_start(
                        out=yt[:], in_=YTs[:, t0:t0 + TT].rearrange("(k p) t -> p k t", p=128)),
                        reads=[YTr[ti]], writes=[yt_r])
                    xr = [] if l == 0 else [XTr[ti]]
                    P.dma("ld", lambda: nc.sync.dma_start(
                        out=x1[:], in_=xsrc[:, t0:t0 + TT].rearrange("(k p) t -> p k t", p=128)),
                        reads=xr, writes=[x1_r])
                    for ob in range(4):
                        wb, wb_r = wget(base + B_OUT + ob)
                        for sub in range(4):
                            m = ob * 4 + sub
                            bk, bk_r = nbank()
                            for kc in range(16):
                                mm(bk[:], wb[:, kc, sub * 128:(sub + 1) * 128], yt[:, kc, :], kc == 0, kc == 15, [wb_r, yt_r], bk_r)
                            P.op("dve", lambda: nc.vector.tensor_tensor(out=x1[:, m, :], in0=bk[:], in1=x1[:, m, :], op=ALU.add),
                                 reads=[bk_r, x1_r], awrites=[x1_r])
                    nb, nb_r = nnbank()
                    for kc in range(16):
                        i = c2["sq"] % 2; c2["sq"] += 1
                        P.op("act", lambda: nc.scalar.activation(out=sqb[i][:], in_=x1[:, kc, :], func=AF.Square),
                             reads=[x1_r], writes=[sqb_r[i]])
                        mm(nb[:], ones_bf[:], sqb[i][:], kc == 0, kc == 15, [sqb_r[i], cres], nb_r)
                    P.op("act", lambda: nc.scalar.activation(out=std[:], in_=nb[:], func=AF.Sqrt,
                                                             bias=epsT[:, 0:1], scale=1.0 / D),
                         reads=[nb_r, cres], writes=[std_r])
                    P.op("dve", lambda: nc.vector.reciprocal(out=rstd[:], in_=std[:]), reads=[std_r], writes=[rstd_r])
                    for kc in range(16):
                        P.op("dve", lambda: nc.vector.scalar_tensor_tensor(
                            out=xn2[:, kc, :], in0=x1[:, kc, :], scalar=gmlpT[:, l, kc:kc + 1], in1=rstd[:],
                            op0=ALU.mult, op1=ALU.mult),
                            reads=[x1_r, rstd_r, pres], awrites=[xn2_r])
                    for qd in range(4):
                        A = actT[qd % 2]; A_r = actT_r[qd % 2]
                        for ub in range(4):
                            wb, wb_r = wget(base + B_UP + qd * 4 + ub)
                            for sub in range(4):
                                f = ub * 4 + sub
                                bk, bk_r = nbank()
                                for kc in range(16):
                                    mm(bk[:], wb[:, kc, sub * 128:(sub + 1) * 128], xn2[:, kc, :], kc == 0, kc == 15,
                                       [wb_r, xn2_r], bk_r)
                                ri = c2["rl"] % 2; c2["rl"] += 1
                                P.op("act", lambda: nc.scalar.activation(out=rl[ri][:], in_=bk[:], func=AF.Relu),
                                     reads=[bk_r], writes=[rl_r[ri]])
                                P.op("pool", lambda: nc.gpsimd.tensor_tensor(out=A[:, f, :], in0=rl[ri][:], in1=rl[ri][:], op=ALU.mult),
                                     reads=[rl_r[ri]], awrites=[A_r])
                        for mw in range(4):
                            wb, wb_r = wget(base + B_DN + qd * 4 + mw)
                            for sub in range(4):
                                m = mw * 4 + sub
                                bk, bk_r = nbank()
                                for fc in range(16):
                                    mm(bk[:], wb[:, fc, sub * 128:(sub + 1) * 128], A[:, fc, :], fc == 0, fc == 15,
                                       [wb_r, A_r], bk_r)
                                P.op("dve", lambda: nc.vector.tensor_tensor(out=x1[:, m, :], in0=bk[:], in1=x1[:, m, :], op=ALU.add),
                                     reads=[bk_r, x1_r], awrites=[x1_r])
                    P.dma("st", lambda: nc.gpsimd.dma_start(
                        out=xdst[:, t0:t0 + TT].rearrange("(k p) t -> p k t", p=128), in_=x1[:]),
                        reads=[x1_r], writes=[XTr[ti]])
                barrier()

        def phase_att(l):
            st5 = contextlib.ExitStack()

            def S5(name, shape, dt=F32):
                return st5.enter_context(nc.sbuf_tensor("%s_a%d" % (name, l), list(shape), dt))

            with st5:
                KT = S5("KT", [128, 4, L], BF16); kt_r = Res()
                Vx = S5("Vx", [128, NCH, 4, 130], BF16); vx_r = Res()
                KI = [S5("KI%d" % k_, [128, L], BF16) for k_ in range(2)]; ki_r = Res()
                dwall = S5("dwall", [128, NCH, 32]); dw_r = Res()
                MBq = S5("MBq", [128, 128]); zf = S5("zf", [128, 128]); zbf = S5("zbf", [128, 128], BF16)
                negc = S5("negc", [128, 1]); thr_c = S5("thr_c", [128, 1])
                k_r = Res()
                for h in range(4):
                    P.dma("ld", lambda: nc.sync.dma_start(out=KT[:, h, :], in_=KTs[h * 128:(h + 1) * 128, :]),
                          reads=KTr, awrites=[kt_r])
                P.op("pool", lambda: nc.gpsimd.memset(KI[0][64:128, :], 0.0), awrites=[ki_r])
                P.op("pool", lambda: nc.gpsimd.memset(KI[1][0:64, :], 0.0), awrites=[ki_r])
                P.dma("ld", lambda: nc.sync.dma_start(out=KI[0][0:64, :], in_=KITs[0:64, :]), reads=KITr, awrites=[ki_r])
                P.dma("ld", lambda: nc.sync.dma_start(out=KI[1][64:128, :], in_=KITs[64:128, :]), reads=KITr, awrites=[ki_r])
                P.dma("ld", lambda: nc.sync.dma_start(out=dwall[:], in_=DW.rearrange("(c p) j -> p c j", p=128)),
                      reads=DWr, writes=[dw_r])
                for c0 in range(NCH):
                    P.dma("ld", lambda: nc.sync.dma_start(
                        out=Vx[:, c0, :, 0:128],
                        in_=Vs[c0 * 128:(c0 + 1) * 128, :].rearrange("p (h d) -> p h d", d=128)),
                        reads=Vr, awrites=[vx_r])

                zf_r = Res()
                P.op("pool", lambda: nc.gpsimd.memset(Vx[:, :, :, 128:130], 1.0), awrites=[vx_r])
                P.op("pool", lambda: nc.gpsimd.memset(zf[:], 0.0), writes=[zf_r])
                P.op("pool", lambda: nc.gpsimd.memset(zbf[:], 0.0), awrites=[k_r])
                P.op("pool", lambda: nc.gpsimd.memset(negc[:], -8.0), awrites=[k_r])
                P.op("pool", lambda: nc.gpsimd.memset(thr_c[:], -1.0e29), awrites=[k_r])
                P.op("pool", lambda: nc.gpsimd.affine_select(out=MBq[:], in_=zf[:], pattern=[[-1, 128]], compare_op=ALU.is_ge,
                                                             fill=NEG, base=0, channel_multiplier=1), reads=[zf_r], awrites=[k_r])
                qt = [S5("qt%d" % i, [128, 8, 128], BF16) for i in range(3)]; qt_r = [Res(), Res(), Res()]
                qit = [S5("qit%d" % i, [128, 8, 128], BF16) for i in range(2)]; qit_r = [Res(), Res()]
                scs = [S5("sc%d" % i, [128, L]) for i in range(2)]; sc_rs = [Res(), Res()]
                sel = S5("sel", [128, L], BF16); sel_r = Res()
                selT = S5("selT", [128, NCH, 128], BF16); selT_r = Res()
                m8 = S5("m8", [128, 8]); m8_r = Res()
                pT = [S5("pT%d" % i, [128, 2, 128], BF16) for i in range(3)]; pT_r = [Res(), Res(), Res()]
                nbig = S5("nbig", [128, 1])
                P.op("pool", lambda: nc.gpsimd.memset(nbig[:], -30000.0), awrites=[k_r])
                rd = S5("rd", [128, 2, 1]); rd_r = Res()
                yatt = S5("yatt", [128, 1024], BF16); yatt_r = Res()
                yat = S5("yat", [128, 8, 512], BF16); yat_r = Res()
                ca = {"rl": 0, "sb": 0, "lb": 0, "p": 0, "ob": 0}
                selTs = [selT, S5("selT1", [128, NCH, 128], BF16)]; selT_rs = [selT_r, Res()]
                KB = 20
                sbk = [pb[0], pb[1], ptrs[1][:, :].bitcast(F32)]
                sbk_r = [pbr[0], pbr[1], ptr_rr[1]]
                tb_only0[0] = True
                Dg = [S5("Dg%d" % k_, [128, 16, 128], BF16) for k_ in range(2)]; Dg_r = [Res(), Res()]
                rlb = [S5("rlb%d" % k_, [128, 512], BF16) for k_ in range(3)]; rlb_r = [Res() for _ in range(3)]
                pw = S5("pw", [128, KB + 1]); Wt = S5("Wt", [128, KB + 1])
                bhi = S5("bhi", [128, 1]); blo = S5("blo", [128, 1]); mid = S5("mid", [128, 1])
                cnt = S5("cnt", [128, 1]); dd = S5("dd", [128, 1]); bs_r = Res()

                for kk in range(KB + 1):
                    P.op("pool", lambda: nc.gpsimd.memset(pw[:, kk:kk + 1], float(2.0 ** -(kk + 1))), awrites=[k_r])
                osb = [S5("osb%d" % k_, [128, 260]) for k_ in range(4)]; osb_r = [Res() for _ in range(4)]

                def idx(i):
                    nk = (i + 1) * 128
                    ts_ = slice(i * 128, (i + 1) * 128)
                    ti = i // 4
                    sc = scs[i % 2]; sc_r = sc_rs[i % 2]
                    Q_, Q_r, QI_, QI_r = qt[i % 3], qt_r[i % 3], qit[i % 2], qit_r[i % 2]
                    P.dma("ld", lambda: nc.sync.dma_start(out=Q_[:], in_=QTs[:, ts_].rearrange("(h p) t -> p h t", p=128)),
                          reads=[QTr[ti]], writes=[Q_r])
                    P.dma("ld", lambda: nc.sync.dma_start(out=QI_[:], in_=QITs[:, ts_].rearrange("(h p) t -> p h t", p=128)),
                          reads=[QITr[ti]], writes=[QI_r])
                    Dg_ = Dg[i % 2]; Dg_r_ = Dg_r[i % 2]
                    P.op("dve", lambda: nc.vector.tensor_tensor(
                        out=Dg_[:], in0=ident[:, None, :].broadcast_to([128, 16, 128]),
                        in1=dwall[:, i, 16:32, None].broadcast_to([128, 16, 128]), op=ALU.mult),
                        reads=[cres, dw_r], writes=[Dg_r_])
                    isteps = [(k0, min(512, nk - k0), h) for k0 in range(0, nk, 512) for h in range(16)]
                    sb_of = {}

                    def emitS(s_):
                        k0, kn, h = isteps[s_]
                        sbi = ca["sb"] % 3; ca["sb"] += 1
                        sb_of[s_] = sbi
                        mm(sbk[sbi][:, 0:kn], QI_[:, h // 2, :], KI[h % 2][:, k0:k0 + kn], True, True,
                           [QI_r, ki_r], sbk_r[sbi])

                    emitS(0)
                    if len(isteps) > 1:
                        emitS(1)
                    for s_, (k0, kn, h) in enumerate(isteps):
                        if s_ + 2 < len(isteps):
                            emitS(s_ + 2)
                        sbi = sb_of.pop(s_)
                        ri = ca["rl"] % 3; ca["rl"] += 1
                        P.op("act", lambda: nc.scalar.activation(out=rlb[ri][:, 0:kn], in_=sbk[sbi][:, 0:kn], func=AF.Relu),
                             reads=[sbk_r[sbi]], writes=[rlb_r[ri]])
                        mm(pb[5][:, 0:kn], Dg_[:, h, :], rlb[ri][:, 0:kn], h == 0, h == 15, [Dg_r_, rlb_r[ri]], pbr[5])
                        if h == 15:
                            P.op("act", lambda: nc.scalar.copy(out=sc[:, k0:k0 + kn], in_=pb[5][:, 0:kn]), reads=[pbr[5]], awrites=[sc_r])

                def bisect(i):
                    nk = (i + 1) * 128
                    ts_ = slice(i * 128, (i + 1) * 128)
                    sc = scs[i % 2]; sc_r = sc_rs[i % 2]
                    work = sel; work_r = sel_r
                    P.op("dve", lambda: nc.vector.tensor_tensor(out=sc[:, ts_], in0=sc[:, ts_], in1=MBq[:], op=ALU.add),
                         reads=[sc_r, k_r], awrites=[sc_r])
                    if i <= 1:
                        thr = thr_c[:, 0:1]
                    else:
                        P.op("dve", lambda: nc.vector.tensor_reduce(out=bhi[:], in_=sc[:, 0:nk], axis=AX.X, op=ALU.max),
                             reads=[sc_r], writes=[bs_r])
                        P.op("dve", lambda: nc.vector.tensor_reduce(out=blo[:], in_=sc[:, 0:i * 128], axis=AX.X, op=ALU.min),
                             reads=[sc_r], awrites=[bs_r])
                        P.op("dve", lambda: nc.vector.tensor_tensor(out=bhi[:], in0=bhi[:], in1=blo[:], op=ALU.subtract),
                             reads=[bs_r], awrites=[bs_r])
                        P.op("dve", lambda: nc.vector.tensor_scalar(out=Wt[:], in0=pw[:], scalar1=bhi[:, 0:1], scalar2=None, op0=ALU.mult),
                             reads=[bs_r, k_r], awrites=[bs_r])
                        P.op("dve", lambda: nc.vector.tensor_tensor(out=mid[:], in0=blo[:], in1=Wt[:, 0:1], op=ALU.add),
                             reads=[bs_r], awrites=[bs_r])
                        for kk in range(KB):
                            P.op("dve", lambda: nc.vector.tensor_scalar(
                                out=work[:, 0:nk], in0=sc[:, 0:nk], scalar1=mid[:, 0:1], scalar2=None, op0=ALU.is_ge, op1=ALU.add,
                                accum_out=cnt[:, 0:1]), reads=[sc_r, bs_r], awrites=[work_r, bs_r])
                            P.op("dve", lambda: nc.vector.tensor_scalar(out=dd[:], in0=cnt[:], scalar1=255.5, scalar2=0.5,
                                                                        op0=ALU.is_gt, op1=ALU.subtract),
                                 reads=[bs_r], awrites=[bs_r])
                            P.op("dve", lambda: nc.vector.scalar_tensor_tensor(out=mid[:], in0=dd[:], scalar=Wt[:, kk:kk + 1], in1=mid[:],
                                                                               op0=ALU.mult, op1=ALU.add),
                                 reads=[bs_r], awrites=[bs_r])
                        P.op("dve", lambda: nc.vector.scalar_tensor_tensor(out=m8[:, 7:8], in0=Wt[:, KB:KB + 1], scalar=-3.0, in1=mid[:],
                                                                           op0=ALU.mult, op1=ALU.add),
                             reads=[bs_r], writes=[m8_r])
                        thr = m8[:, 7:8]
                    P.op("dve", lambda: nc.vector.tensor_scalar(out=sel[:, 0:nk], in0=sc[:, 0:nk], scalar1=thr, scalar2=None, op0=ALU.is_ge),
                         reads=[sc_r, m8_r, k_r], awrites=[sel_r])

                def selT_build(i):
                    ST, ST_r = selTs[i % 2], selT_rs[i % 2]
                    for j0 in range(0, i + 1, 4):
                        nj = min(4, i + 1 - j0)
                        tb, tb_r = tbank()
                        for jj in range(nj):
                            j = j0 + jj
                            P.op("pe", lambda: nc.tensor.transpose(out=tb[:, jj * 128:(jj + 1) * 128], in_=sel[:, j * 128:(j + 1) * 128],
                                                                   identity=ident[:]),
                                 reads=[sel_r, cres], awrites=[tb_r])
                        P.op("act", lambda: nc.scalar.activation(out=ST[:, j0:j0 + nj, :],
                                                                 in_=tb[:, 0:nj * 128].rearrange("p (k t) -> p k t", t=128),
                                                                 func=AF.Identity, bias=nbig[:, 0:1], scale=30000.0),
                             reads=[tb_r, k_r], awrites=[ST_r])

                def attn(i):
                    Q_, Q_r = qt[i % 3], qt_r[i % 3]
                    ST, ST_r = selTs[i % 2], selT_rs[i % 2]
                    steps = [(kvh, j) for kvh in range(4) for j in range(i + 1)]
                    lbs = {}

                    def emitL(s_):
                        kvh, j = steps[s_]
                        lb = 2 + ca["lb"] % 2; ca["lb"] += 1
                        lbs[s_] = lb
                        mm(pb[lb][:, 0:256], KT[:, kvh, j * 128:(j + 1) * 128],
                           Q_[:, 2 * kvh:2 * kvh + 2, :].rearrange("p h t -> p (h t)"), True, False, [kt_r, Q_r], pbr[lb])
                        mm(pb[lb][:, 0:256], ident[:], ST[:, j, None, :].broadcast_to([128, 2, 128]), False, True, [cres, ST_r], pbr[lb])

                    emitL(0)
                    O = O_r = None
                    for s_, (kvh, j) in enumerate(steps):
                        if s_ + 1 < len(steps):
                            emitL(s_ + 1)
                        lb = lbs.pop(s_)
                        if j == 0:
                            ob = 4
                            O = pb[ob]; O_r = pbr[ob]
                            mm(O[:, 0:260], zbf[:], KT[:, 0, 0:260], True, False, [k_r, kt_r], O_r)
                        pi = ca["p"] % 3; ca["p"] += 1
                        P.op("act", lambda: nc.scalar.activation(out=pT[pi][:].rearrange("p h t -> p (h t)"), in_=pb[lb][:, 0:256],
                                                                 func=AF.Exp, bias=negc[:, 0:1], scale=1.0),
                             reads=[pbr[lb], k_r], writes=[pT_r[pi]])
                        for hh in range(2):
                            mm(O[:, hh * 130:hh * 130 + 129], pT[pi][:, hh, :], Vx[:, j, kvh, 0:129], False,
                               (j == i and hh == 1), [pT_r[pi], vx_r], O_r)
                        if j == i:
                            P.op("act", lambda: nc.scalar.copy(out=osb[kvh][:], in_=O[:, 0:260]), reads=[O_r], writes=[osb_r[kvh]])

                def finalize(i):
                    ti = i // 4
                    for kvh in range(4):
                        Ov = osb[kvh][:].rearrange("p (h d) -> p h d", d=130)
                        P.op("dve", lambda: nc.vector.reciprocal(out=rd[:], in_=Ov[:, :, 128:129]), reads=[osb_r[kvh]], writes=[rd_r])
                        for hh in range(2):
                            hd = 2 * kvh + hh
                            P.op("dve", lambda: nc.vector.tensor_scalar(
                                out=yatt[:, hd * 128:(hd + 1) * 128], in0=osb[kvh][:, hh * 130:hh * 130 + 128], scalar1=rd[:, hh, :],
                                scalar2=None, op0=ALU.mult),
                                reads=[osb_r[kvh], rd_r], awrites=[yatt_r])
                    for half in range(2):
                        tb, tb_r = tbank()
                        for q4 in range(4):
                            hd = half * 4 + q4
                            P.op("pe", lambda: nc.tensor.transpose(out=tb[:, q4 * 128:(q4 + 1) * 128], in_=yatt[:, hd * 128:(hd + 1) * 128],
                                                                   identity=ident[:]),
                                 reads=[yatt_r, cres], awrites=[tb_r])
                        P.op("act", lambda: nc.scalar.copy(out=yat[:, half * 4:half * 4 + 4, (i % 4) * 128:(i % 4 + 1) * 128],
                                                           in_=tb.rearrange("p (k t) -> p k t", t=128)),
                             reads=[tb_r], awrites=[yat_r])
                    if i % 4 == 3:
                        P.dma("st", lambda: nc.gpsimd.dma_start(
                            out=YTs[1024:2048, ti * 512:(ti + 1) * 512].rearrange("(k p) t -> p k t", p=128), in_=yat[:]),
                            reads=[yat_r], awrites=[YTr[ti]])

                idx(0)
                bisect(0)
                selT_build(0)
                if NCH > 1:
                    idx(1)
                for i in range(NCH):
                    pump(3)
                    if i + 2 < NCH:
                        idx(i + 2)
                    if i + 1 < NCH:
                        bisect(i + 1)
                    attn(i)
                    finalize(i)
                    if i + 1 < NCH:
                        selT_build(i + 1)
                tb_only0[0] = False
                barrier()

        def phase_ssd(l):
            st3 = contextlib.ExitStack()

            def S3(name, shape, dt=F32):
                return st3.enter_context(nc.sbuf_tensor("%s_s%d" % (name, l), list(shape), dt))

            with st3:
                NC16 = NCH * 16
                BT = S3("BT", [128, 2, L], BF16); CT = S3("CT", [128, 2, L], BF16); bc_r = Res()
                dwall = S3("dwall", [128, NCH, 32]); dw_r = Res()
                UTf = S3("UTf", [128, 128]); MB = S3("MB", [128, 128]); zf = S3("zf", [128, 128])
                dtb = S3("dtb", [128, 16]); alb = S3("alb", [128, 16]); dsk = S3("dsk", [128, 16])
                gsb = S3("gsb", [128, 1024])
                k_r = Res()
                for g in range(2):
                    P.dma("ld", lambda: nc.sync.dma_start(out=BT[:, g, :], in_=BCT[g * 128:(g + 1) * 128, :]),
                          reads=BCTr, awrites=[bc_r])
                    P.dma("ld", lambda: nc.sync.dma_start(out=CT[:, g, :], in_=BCT[256 + g * 128:256 + (g + 1) * 128, :]),
                          reads=BCTr, awrites=[bc_r])
                P.dma("ld", lambda: nc.sync.dma_start(out=dwall[:], in_=DW.rearrange("(c p) j -> p c j", p=128)),
                      reads=DWr, writes=[dw_r])
                for dst, src in ((dtb, dt_bias), (alb, a_log), (dsk, d_skip), (gsb, g_ssd)):
                    P.dma("ld", lambda: nc.sync.dma_start(out=dst[:], in_=src[l].partition_broadcast(128)),
                          awrites=[k_r])

                zf_r = Res()
                P.op("pool", lambda: nc.gpsimd.memset(zf[:], 0.0), writes=[zf_r])
                P.op("pool", lambda: nc.gpsimd.affine_select(out=UTf[:], in_=ones_f[:], pattern=[[1, 128]], compare_op=ALU.is_ge,
                                                             fill=0.0, base=0, channel_multiplier=-1), reads=[cres], awrites=[k_r])
                P.op("pool", lambda: nc.gpsimd.affine_select(out=MB[:], in_=zf[:], pattern=[[1, 128]], compare_op=ALU.is_ge,
                                                             fill=-30000.0, base=0, channel_multiplier=-1), reads=[zf_r], awrites=[k_r])
                LIM = int(os.environ.get("SSD_LIM", "99"))

                class _Stop(Exception):
                    pass

                def ck(n):
                    if LIM <= n:
                        raise _Stop()
                ssd_body(S3, locals())
                barrier()

        def ssd_body(S3, env):
            globals_ = env
            try:
                _ssd_body(S3, env)
            except Exception as e:
                if type(e).__name__ != "_Stop":
                    raise

        def _ssd_body(S3, env):
            if True:
                (BT, CT, bc_r, dwall, dw_r, UTf, MB, dtb, alb, dsk, gsb, k_r, ck, l) = [env[n] for n in
                    ("BT", "CT", "bc_r", "dwall", "dw_r", "UTf", "MB", "dtb", "alb", "dsk", "gsb", "k_r", "ck", "l")]
                NC16 = NCH * 16
                ck(0)
                dtx = S3("dtx", [128, NCH, 16]); t_a = S3("t_a", [128, NCH, 16]); t_b = S3("t_b", [128, NCH, 16])
                dtv = S3("dtv", [128, NCH, 16]); adt = S3("adt", [128, NCH, 16]); cum = S3("cum", [128, NCH, 16])
                tot = S3("tot", [128, NCH, 16]); eal = S3("eal", [128, NCH, 16]); wse = S3("wse", [128, NCH, 16])
                cdv = S3("cdv", [128, NCH, 16]); eab = S3("eab", [128, 16])
                pr_r = Res()
                bc16 = lambda t: t[:, None, :].broadcast_to([128, NCH, 16])
                P.op("dve", lambda: nc.vector.tensor_tensor(out=dtx[:], in0=dwall[:, :, 0:16], in1=bc16(dtb), op=ALU.add),
                     reads=[dw_r, k_r], writes=[pr_r])
                P.op("act", lambda: nc.scalar.activation(out=t_a[:], in_=dtx[:], func=AF.Abs),
                     reads=[pr_r], awrites=[pr_r])
                P.op("act", lambda: nc.scalar.activation(out=t_a[:], in_=t_a[:], func=AF.Exp, scale=-1.0),
                     reads=[pr_r], awrites=[pr_r])
                P.op("act", lambda: nc.scalar.activation(out=t_a[:], in_=t_a[:], func=AF.Ln, bias=ones_f[:, 0:1], scale=1.0),
                     reads=[pr_r, cres], awrites=[pr_r])
                P.op("dve", lambda: nc.vector.tensor_scalar(out=t_b[:], in0=dtx[:], scalar1=0.0, scalar2=None, op0=ALU.max),
                     reads=[pr_r], awrites=[pr_r])
                P.op("dve", lambda: nc.vector.tensor_tensor(out=dtv[:], in0=t_a[:], in1=t_b[:], op=ALU.add),
                     reads=[pr_r], awrites=[pr_r])
                P.op("act", lambda: nc.scalar.activation(out=eab[:], in_=alb[:], func=AF.Exp), reads=[k_r], awrites=[pr_r])
                P.op("dve", lambda: nc.vector.scalar_tensor_tensor(out=adt[:], in0=dtv[:], scalar=-1.0, in1=bc16(eab),
                                                                   op0=ALU.mult, op1=ALU.mult),
                     reads=[pr_r], awrites=[pr_r])
                adt_f = adt[:].rearrange("p c h -> p (c h)")
                cum_f = cum[:].rearrange("p c h -> p (c h)")
                tot_f = tot[:].rearrange("p c h -> p (c h)")
                for o in range(0, NC16, 512):
                    n = min(512, NC16 - o)
                    mm(pb[0][:, 0:n], UTf[:], adt_f[:, o:o + n], True, True, [pr_r, k_r], pbr[0])
                    P.op("act", lambda: nc.scalar.copy(out=cum_f[:, o:o + n], in_=pb[0][:, 0:n]), reads=[pbr[0]], awrites=[pr_r])
                    mm(pb[1][:, 0:n], ones_f[:], adt_f[:, o:o + n], True, True, [pr_r, cres], pbr[1])
                    P.op("act", lambda: nc.scalar.copy(out=tot_f[:, o:o + n], in_=pb[1][:, 0:n]), reads=[pbr[1]], awrites=[pr_r])
                P.op("act", lambda: nc.scalar.activation(out=eal[:], in_=cum[:], func=AF.Exp), reads=[pr_r], awrites=[pr_r])
                P.op("act", lambda: nc.scalar.activation(out=cdv[:], in_=tot[:], func=AF.Exp), reads=[pr_r], awrites=[pr_r])
                P.op("dve", lambda: nc.vector.tensor_tensor(out=wse[:], in0=tot[:], in1=cum[:], op=ALU.subtract),
                     reads=[pr_r], awrites=[pr_r])
                P.op("act", lambda: nc.scalar.activation(out=wse[:], in_=wse[:], func=AF.Exp), reads=[pr_r], awrites=[pr_r])
                ck(1)
                xs_t = [S3("xs_t%d" % i, [128, 1024], BF16) for i in range(2)]; xs_r = [Res(), Res()]
                z_t = [S3("z_t%d" % i, [128, 1024], BF16) for i in range(2)]; z_r = [Res(), Res()]
                b_t = [S3("b_t%d" % i, [128, 256], BF16) for i in range(2)]; b_r = [Res(), Res()]
                Rm = S3("Rm", [128, 16, 128]); Rm_r = Res()
                NM = S3("NM", [128, 16, 128]); NM_r = Res()
                seg = S3("seg", [128, 16, 128]); seg_r = Res()
                MT = S3("MT", [128, 16, 128], BF16); MT_r = Res()
                xdt = S3("xdt", [128, 16, 64], BF16); xdt_r = Res()
                xdtw = S3("xdtw", [128, 16, 64], BF16); xdtw_r = Res()
                yv = S3("yv", [128, 1024]); yv_r = Res()
                t2 = S3("t2", [128, 1024]); t2_r = Res()
                sz = S3("sz", [128, 1024]); sz_r = Res()
                junk = S3("junk", [128, 512]); junk_r = Res()
                ss = S3("ss", [128, 2]); ss_r = Res()
                sd = S3("sd", [128, 2]); rs = S3("rs", [128, 2]); rs_r = Res()
                yn = S3("yn", [128, 1024], BF16); yn_r = Res()
                ytt = S3("ytt", [128, 8, 512], BF16); ytt_r = Res()
                Sf = S3("Sf", [128, 1024]); Sf_r = Res()
                Sbf = S3("Sbf", [128, 1024], BF16); Sbf_r = Res()
                v3 = lambda t: t.rearrange("p (h d) -> p h d", d=64)
                for c in range(NCH):
                    cs = slice(c * 128, (c + 1) * 128)
                    bi = c % 2
                    ti = c // 4
                    X_, Z_, B_ = xs_t[bi], z_t[bi], b_t[bi]
                    P.dma("ld", lambda: nc.sync.dma_start(out=X_[:], in_=XSs[cs, :]), reads=[XSr[ti]], writes=[xs_r[bi]])
                    P.dma("ld", lambda: nc.sync.dma_start(out=Z_[:], in_=Zs[cs, :]), reads=[Zr[ti]], writes=[z_r[bi]])
                    P.dma("ld", lambda: nc.sync.dma_start(out=B_[:], in_=Bs[cs, :]), reads=[Br[ti]], writes=[b_r[bi]])
                    P.op("pool", lambda: nc.gpsimd.tensor_tensor(
                        out=Rm[:], in0=UTf[:, None, :].broadcast_to([128, 16, 128]),
                        in1=adt[:, c, :, None].broadcast_to([128, 16, 128]), op=ALU.mult),
                        reads=[pr_r, k_r], writes=[Rm_r])
                    P.op("pool", lambda: nc.gpsimd.tensor_tensor(
                        out=NM[:], in0=MB[:, None, :].broadcast_to([128, 16, 128]),
                        in1=cum[:, c, :, None].broadcast_to([128, 16, 128]), op=ALU.subtract),
                        reads=[pr_r, k_r], writes=[NM_r])
                    for j in range(4):
                        mm(pb[j][:], ones_f[:], Rm[:, 4 * j:4 * j + 4, :].rearrange("p h l -> p (h l)"), True, True,
                           [Rm_r, cres], pbr[j])
                        P.op("dve", lambda: nc.vector.tensor_tensor(
                            out=seg[:, 4 * j:4 * j + 4, :].rearrange("p h l -> p (h l)"), in0=pb[j][:],
                            in1=NM[:, 4 * j:4 * j + 4, :].rearrange("p h l -> p (h l)"), op=ALU.add),
                            reads=[pbr[j], NM_r], awrites=[seg_r])
                    P.op("act", lambda: nc.scalar.activation(out=seg[:], in_=seg[:], func=AF.Exp), reads=[seg_r], awrites=[seg_r])
                    ck(2)
                    for g in range(2):
                        mm(pb[4][:, g * 128:(g + 1) * 128], BT[:, g, cs], CT[:, g, cs], True, True, [bc_r], pbr[4])
                    for g in range(2):
                        P.op("dve", lambda: nc.vector.tensor_tensor(
                            out=MT[:, 8 * g:8 * g + 8, :], in0=seg[:, 8 * g:8 * g + 8, :],
                            in1=pb[4][:, None, g * 128:(g + 1) * 128].broadcast_to([128, 8, 128]), op=ALU.mult),
                            reads=[seg_r, pbr[4]], awrites=[MT_r])
                    ck(3)
                    P.op("pool", lambda: nc.gpsimd.tensor_tensor(
                        out=xdt[:], in0=v3(X_[:]), in1=dtv[:, c, :, None].broadcast_to([128, 16, 64]), op=ALU.mult),
                        reads=[xs_r[bi], pr_r], writes=[xdt_r])
                    P.op("pool", lambda: nc.gpsimd.tensor_tensor(
                        out=xdtw[:], in0=xdt[:], in1=wse[:, c, :, None].broadcast_to([128, 16, 64]), op=ALU.mult),
                        reads=[xdt_r, pr_r], writes=[xdtw_r])
                    for h in range(16):
                        bk = pb[5 - h // 8]
                        mm(bk[:, (h % 8) * 64:(h % 8 + 1) * 64], MT[:, h, :], xdt[:, h, :], True, True, [MT_r, xdt_r], pbr[5 - h // 8])
                    ck(4)
                    if c > 0:
                        for g in range(2):
                            mm(pb[g][:], CT[:, g, cs], Sbf[:, g * 512:(g + 1) * 512], True, True, [bc_r, Sbf_r], pbr[g])
                        for g in range(2):
                            P.op("dve", lambda: nc.vector.tensor_tensor(
                                out=v3(yv[:, g * 512:(g + 1) * 512]), in0=v3(pb[g][:]),
                                in1=eal[:, c, 8 * g:8 * g + 8, None].broadcast_to([128, 8, 64]), op=ALU.mult),
                                reads=[pbr[g], pr_r], awrites=[yv_r])
                            P.op("dve", lambda: nc.vector.tensor_tensor(
                                out=yv[:, g * 512:(g + 1) * 512], in0=pb[5 - g][:], in1=yv[:, g * 512:(g + 1) * 512], op=ALU.add),
                                reads=[pbr[5 - g], yv_r], awrites=[yv_r])
                    else:
                        for g in range(2):
                            P.op("dve", lambda: nc.vector.tensor_copy(out=yv[:, g * 512:(g + 1) * 512], in_=pb[5 - g][:]),
                                 reads=[pbr[5 - g]], awrites=[yv_r])
                    P.op("pool", lambda: nc.gpsimd.tensor_tensor(
                        out=v3(t2[:]), in0=v3(X_[:]), in1=dsk[:, :, None].broadcast_to([128, 16, 64]), op=ALU.mult),
                        reads=[xs_r[bi], k_r], writes=[t2_r])
                    P.op("dve", lambda: nc.vector.tensor_tensor(out=yv[:], in0=yv[:], in1=t2[:], op=ALU.add),
                         reads=[yv_r, t2_r], awrites=[yv_r])
                    P.op("act", lambda: nc.scalar.activation(out=sz[:], in_=Z_[:], func=AF.Silu), reads=[z_r[bi]], writes=[sz_r])
                    P.op("dve", lambda: nc.vector.tensor_tensor(out=yv[:], in0=yv[:], in1=sz[:], op=ALU.mult),
                         reads=[yv_r, sz_r], awrites=[yv_r])
                    for g in range(2):
                        P.op("act", lambda: nc.scalar.activation(out=junk[:], in_=yv[:, g * 512:(g + 1) * 512], func=AF.Square,
                                                                 accum_out=ss[:, g:g + 1]),
                             reads=[yv_r], writes=[junk_r], awrites=[ss_r])
                    P.op("act", lambda: nc.scalar.activation(out=sd[:], in_=ss[:], func=AF.Sqrt, bias=epsT[:, 0:1], scale=1.0 / 512),
                         reads=[ss_r, cres], awrites=[rs_r])
                    P.op("dve", lambda: nc.vector.reciprocal(out=rs[:], in_=sd[:]), reads=[rs_r], awrites=[rs_r])
                    for g in range(2):
                        P.op("dve", lambda: nc.vector.scalar_tensor_tensor(
                            out=yn[:, g * 512:(g + 1) * 512], in0=yv[:, g * 512:(g + 1) * 512], scalar=rs[:, g:g + 1],
                            in1=gsb[:, g * 512:(g + 1) * 512], op0=ALU.mult, op1=ALU.mult),
                            reads=[yv_r, rs_r, k_r], awrites=[yn_r])
                    ck(5)
                    SV = int(os.environ.get("SSD_V", "0"))
                    for half in range(2):
                        tb, tb_r = tbank()
                        for q4 in range(4):
                            kc = half * 4 + q4
                            if SV == 2:
                                continue
                            P.op("pe", lambda: nc.tensor.transpose(out=tb[:, q4 * 128:(q4 + 1) * 128], in_=yn[:, kc * 128:(kc + 1) * 128],
                                                                   identity=ident[:]),
                                 reads=[yn_r, cres], awrites=[tb_r])
                        if SV == 1:
                            continue
                        P.op("act", lambda: nc.scalar.copy(out=ytt[:, half * 4:half * 4 + 4, (c % 4) * 128:(c % 4 + 1) * 128],
                                                           in_=tb.rearrange("p (k t) -> p k t", t=128)),
                             reads=[tb_r], awrites=[ytt_r])
                    if c % 4 == 3:
                        P.dma("st", lambda: nc.gpsimd.dma_start(
                            out=YTs[0:1024, ti * 512:(ti + 1) * 512].rearrange("(k p) t -> p k t", p=128), in_=ytt[:]),
                            reads=[ytt_r], awrites=[YTr[ti]])
                    ck(6)
                    if c < NCH - 1:
                        for g in range(2):
                            mm(pb[2 + g][:], B_[:, g * 128:(g + 1) * 128], xdtw[:, 8 * g:8 * g + 8, :].rearrange("p h d -> p (h d)"),
                               True, True, [b_r[bi], xdtw_r], pbr[2 + g])
                        if c > 0:
                            P.op("dve", lambda: nc.vector.tensor_tensor(
                                out=v3(Sf[:]), in0=v3(Sf[:]), in1=cdv[:, c, :, None].broadcast_to([128, 16, 64]), op=ALU.mult),
                                reads=[Sf_r, pr_r], awrites=[Sf_r])
                            for g in range(2):
                                P.op("dve", lambda: nc.vector.tensor_tensor(
                                    out=Sf[:, g * 512:(g + 1) * 512], in0=pb[2 + g][:], in1=Sf[:, g * 512:(g + 1) * 512], op=ALU.add),
                                    reads=[pbr[2 + g], Sf_r], awrites=[Sf_r])
                        else:
                            for g in range(2):
                                P.op("dve", lambda: nc.vector.tensor_copy(out=Sf[:, g * 512:(g + 1) * 512], in_=pb[2 + g][:]),
                                     reads=[pbr[2 + g]], awrites=[Sf_r])
                        P.op("act", lambda: nc.scalar.copy(out=Sbf[:], in_=Sf[:]), reads=[Sf_r], writes=[Sbf_r])
                barrier()

        barrier()
        for l in range(NL):
            phase_in(l)
            if stop_after == "in":
                break
            phase_ssd(l)
            if stop_after == "ssd":
                break
            if stop_after != "noatt":
                phase_att(l)
            if stop_after == "att":
                break
            phase_out(l)

        P.finish()
        sems = {}
        for e in COMPUTE:
            sems[e] = stack.enter_context(nc.semaphore("s_" + e))
        for q, Q in P.dmaq.items():
            for s in range(Q["k"]):
                sems[(q, s)] = stack.enter_context(nc.semaphore("s_%s%d" % (q, s)))
        P.emit(sems)
    return real_nc


_NC_CACHE = {}
_PNAMES = ["w_in", "w_out", "w_up", "w_down", "g_mix", "g_mlp", "conv_w", "conv_b", "dt_bias", "a_log",
           "d_skip", "g_ssd", "g_q", "g_k", "g_kidx"]


def kernel(**inputs):
    x = np.asarray(inputs["x"], dtype=np.float32)
    Bn, L, _ = x.shape
    NL = int(np.asarray(inputs["w_in"]).shape[0])
    key = (L, NL)
    if key not in _NC_CACHE:
        _NC_CACHE[key] = build(L=L, NL=NL)
    nc = _NC_CACHE[key]
    common = {n: np.ascontiguousarray(np.asarray(inputs[n], dtype=np.float32)) for n in _PNAMES}
    in_maps = [dict(common, xT=np.ascontiguousarray(x[b].T)) for b in range(Bn)]
    res = run_bass_kernel_spmd(nc, in_maps, core_ids=list(range(Bn)))
    out = np.stack([np.asarray(res.results[b]["outT"]).T for b in range(Bn)], axis=0)
    return np.ascontiguousarray(out.astype(np.float32))
```
